# Optimizing a Trainium2 kernel written in Bass

```python
import jax, jax.numpy as jnp
from jax import lax
import numpy as np

D_MODEL = 1024
BATCH = 8
SEQ = 4096
DEPTH = 2

N_HEADS = 16
HEAD_DIM = D_MODEL // N_HEADS
Q_BLOCK = 128
POOL_WINDOWS = (2, 4, 8, 16)
N_POOL_GROUPS = len(POOL_WINDOWS)
POOL_GROUP = D_MODEL // N_POOL_GROUPS
D_FF = 2816
N_MOD = 9
N_MIXERS = 2
N_ATTN_LAYERS = (DEPTH + 1) // 2
N_POOL_LAYERS = DEPTH // 2
EPS = 1e-6

kernel_name = "hybrid_stickbreak_pool_macaron_adaln"


def rmsnorm(x, g):
    xf = x.astype(jnp.float32)
    y = xf * lax.rsqrt(jnp.mean(xf * xf, axis=-1, keepdims=True) + EPS)
    return (y * g.astype(jnp.float32)).astype(x.dtype)


def modulate(h, shift, scale):
    return h * (1.0 + scale[:, None, :]) + shift[:, None, :]


def swiglu(h, w1, w2):
    gate, up = jnp.split(h @ w1, 2, axis=-1)
    return (jax.nn.silu(gate) * up) @ w2


def stick_breaking_attention(h, w_in, w_out):
    b, s_len, d = h.shape
    qkv = (h @ w_in).reshape(b, s_len, 3, N_HEADS, HEAD_DIM)
    q, k, v = qkv[:, :, 0], qkv[:, :, 1], qkv[:, :, 2]
    inv_sqrt_d = 1.0 / float(np.sqrt(HEAD_DIM))
    outs = []
    for qb in range(s_len // Q_BLOCK):
        t0 = qb * Q_BLOCK
        end = t0 + Q_BLOCK
        qi = q[:, t0:end]
        kj = k[:, :end]
        vj = v[:, :end]
        z = jnp.einsum('bqhd,bkhd->bhqk', qi, kj).astype(jnp.float32) * inv_sqrt_d
        t_pos = t0 + jnp.arange(Q_BLOCK)[:, None]
        s_pos = jnp.arange(end)[None, :]
        causal = s_pos < t_pos
        log_beta = jax.nn.log_sigmoid(z)
        log_one_minus = jnp.where(causal, -jax.nn.softplus(z), 0.0)
        shifted = jnp.concatenate([log_one_minus[..., 1:], jnp.zeros_like(log_one_minus[..., :1])], axis=-1)
        suffix = lax.cumsum(shifted, axis=3, reverse=True)
        a = jnp.where(causal, jnp.exp(log_beta + suffix), 0.0)
        outs.append(jnp.einsum('bhqk,bkhd->bqhd', a.astype(v.dtype), vj))
    o = jnp.concatenate(outs, axis=1).reshape(b, s_len, d)
    return o @ w_out


def multiscale_pool_mixer(h, w_in, w_group, scale, w_out):
    b, s_len, d = h.shape
    u = h @ w_in
    uf = u.astype(jnp.float32)
    cs = jnp.concatenate([jnp.zeros((b, 1, d), jnp.float32), jnp.cumsum(uf, axis=1)], axis=1)
    hi = cs[:, 1:]
    pos = jnp.arange(s_len, dtype=jnp.float32)[None, :, None]
    groups = []
    for gi, w in enumerate(POOL_WINDOWS):
        sl = slice(gi * POOL_GROUP, (gi + 1) * POOL_GROUP)
        lo = jnp.pad(cs[:, :, sl], ((0, 0), (w - 1, 0), (0, 0)))[:, :s_len]
        count = jnp.minimum(pos + 1.0, float(w))
        mean = (hi[:, :, sl] - lo) / count
        groups.append(mean - uf[:, :, sl])
    p = jnp.stack(groups, axis=2).astype(h.dtype)
    p = jnp.einsum('bsgc,gce->bsge', p, w_group).reshape(b, s_len, d)
    return (p * scale) @ w_out


def setup_inputs(seed: int = 0) -> dict:
    key = jax.random.key(seed)
    ks = jax.random.split(key, 16)

    def dense(k, shape, fan_in, gain=1.0):
        return jax.random.normal(k, shape, jnp.float32) * (gain * fan_in ** -0.5)

    x = jax.random.normal(ks[0], (BATCH, SEQ, D_MODEL), jnp.float32)
    c = jax.random.normal(ks[1], (BATCH, D_MODEL), jnp.float32)
    mod_w = dense(ks[2], (DEPTH, D_MODEL, N_MOD * D_MODEL), D_MODEL, 0.1)
    mod_b = 0.01 * jax.random.normal(ks[3], (DEPTH, N_MOD * D_MODEL), jnp.float32)
    norm_g = 1.0 + 0.05 * jax.random.normal(ks[4], (DEPTH, 3, D_MODEL), jnp.float32)
    ffn_w1 = dense(ks[5], (DEPTH, 2, D_MODEL, 2 * D_FF), D_MODEL)
    ffn_w2 = dense(ks[6], (DEPTH, 2, D_FF, D_MODEL), D_FF)
    attn_w_in = dense(ks[7], (N_ATTN_LAYERS, D_MODEL, 3 * D_MODEL), D_MODEL)
    attn_w_out = dense(ks[8], (N_ATTN_LAYERS, D_MODEL, D_MODEL), D_MODEL)
    pool_w_in = dense(ks[9], (N_POOL_LAYERS, D_MODEL, D_MODEL), D_MODEL)
    pool_w_group = dense(ks[10], (N_POOL_LAYERS, N_POOL_GROUPS, POOL_GROUP, POOL_GROUP), POOL_GROUP)
    pool_scale = 1.0 + 0.1 * jax.random.normal(ks[11], (N_POOL_LAYERS, D_MODEL), jnp.float32)
    pool_w_out = dense(ks[12], (N_POOL_LAYERS, D_MODEL, D_MODEL), D_MODEL)
    final_norm = 1.0 + 0.05 * jax.random.normal(ks[13], (D_MODEL,), jnp.float32)
    return {"x": x, "c": c, "mod_w": mod_w, "mod_b": mod_b, "norm_g": norm_g,
            "ffn_w1": ffn_w1, "ffn_w2": ffn_w2,
            "attn_w_in": attn_w_in, "attn_w_out": attn_w_out,
            "pool_w_in": pool_w_in, "pool_w_group": pool_w_group,
            "pool_scale": pool_scale, "pool_w_out": pool_w_out,
            "final_norm": final_norm}


def reference(x, c, mod_w, mod_b, norm_g, ffn_w1, ffn_w2, attn_w_in, attn_w_out,
              pool_w_in, pool_w_group, pool_scale, pool_w_out, final_norm):
    b = x.shape[0]
    c_act = jax.nn.silu(c)
    for i in range(DEPTH):
        mod = (c_act @ mod_w[i] + mod_b[i]).reshape(b, N_MOD, D_MODEL)
        sh1, sc1, g1, sh2, sc2, g2, sh3, sc3, g3 = [mod[:, j] for j in range(N_MOD)]
        h = modulate(rmsnorm(x, norm_g[i, 0]), sh1, sc1)
        x = x + 0.5 * (1.0 + g1)[:, None, :] * swiglu(h, ffn_w1[i, 0], ffn_w2[i, 0])
        h = modulate(rmsnorm(x, norm_g[i, 1]), sh2, sc2)
        j = i // N_MIXERS
        if i % N_MIXERS == 0:
            m = stick_breaking_attention(h, attn_w_in[j], attn_w_out[j])
        else:
            m = multiscale_pool_mixer(h, pool_w_in[j], pool_w_group[j], pool_scale[j], pool_w_out[j])
        x = x + (1.0 + g2)[:, None, :] * m
        h = modulate(rmsnorm(x, norm_g[i, 2]), sh3, sc3)
        x = x + 0.5 * (1.0 + g3)[:, None, :] * swiglu(h, ffn_w1[i, 1], ffn_w2[i, 1])
    return rmsnorm(x, final_norm)
```

```python
import numpy as np
from contextlib import ExitStack
import concourse.bass as bass
import concourse.mybir as mybir
from concourse.bass_utils import run_bass_kernel_spmd

F32 = mybir.dt.float32
BF16 = mybir.dt.bfloat16
AF = mybir.ActivationFunctionType
ALU = mybir.AluOpType

D = 1024
SEQ = 4096
NCORE = 8
DFF = 2816
NQ = DFF // 128
TT = 512
NT = SEQ // TT
EPS = 1e-6
ENGS = ["pe", "act", "dve", "pool", "sp"]


class Op:
    __slots__ = ("eng", "fn", "deps", "sig", "val", "semkey", "ordidx", "isdma")


class Sched:
    DMA_SEMS = {"sp": 24, "pool": 8, "act": 4}

    def __init__(self):
        self.dmacount = {}
        self.q = {e: [] for e in ENGS}
        self.semlist = {}
        self.lastw = {}
        self.rd = {}

    def add(self, eng, fn, reads=(), writes=(), dma=None):
        op = Op()
        op.eng = eng
        op.fn = fn
        op.isdma = dma is not None
        if dma is not None:
            n = self.dmacount.get(eng, 0)
            self.dmacount[eng] = n + 1
            op.semkey = ("dma", eng, n % self.DMA_SEMS[eng])
        else:
            op.semkey = eng
        op.sig = False
        op.val = 0
        deps = {}

        def need(o):
            if o is None:
                return
            if o.semkey == "pe" and op.semkey == "pe":
                return
            k = o.semkey
            if k not in deps or deps[k].ordidx < o.ordidx:
                deps[k] = o

        if op.isdma and self.semlist.get(op.semkey):
            need(self.semlist[op.semkey][-1])
        for r in reads:
            need(self.lastw.get(r))
        for w in writes:
            need(self.lastw.get(w))
            for o in self.rd.get(w, {}).values():
                need(o)
        lst = self.semlist.setdefault(op.semkey, [])
        op.ordidx = len(lst)
        lst.append(op)
        for o in deps.values():
            o.sig = True
        op.deps = list(deps.values())
        for r in reads:
            self.rd.setdefault(r, {})[op.semkey] = op
        for w in writes:
            self.lastw[w] = op
            self.rd[w] = {}
        self.q[eng].append(op)
        return op

    def finalize(self):
        for k, lst in self.semlist.items():
            if isinstance(k, tuple):
                for i, o in enumerate(lst):
                    o.val = 16 * (i + 1)
            else:
                n = 0
                for o in lst:
                    if o.sig:
                        n += 1
                        o.val = n

    def emit_engine(self, eng, e, sems):
        waited = {}
        for op in self.q[eng]:
            for d in op.deps:
                k = d.semkey
                if waited.get(k, 0) >= d.val:
                    continue
                waited[k] = d.val
                e.wait_ge(sems[k], d.val)
            ins = op.fn(e)
            if op.isdma:
                ins.then_inc(sems[op.semkey], 16)
            elif op.sig:
                ins.then_inc(sems[op.semkey], 1)


def build_program(phases=("mod", "L0", "L1", "final"), sub=None):
    nc = bass.Bass("TRN2", target_bir_lowering=False)
    S = Sched()

    def dram_in(name, shape, dt=F32):
        return nc.dram_tensor(name, list(shape), dt, kind="ExternalInput").ap()

    def dram_tmp(name, shape, dt=BF16):
        return nc.dram_tensor(name, list(shape), dt, kind="Internal").ap()

    xT_d = dram_in("xT", [128, 8, SEQ])
    cpk_d = dram_in("cpk", [128, 8])
    vec_d = dram_in("vec", [128, 208])
    mw_d = dram_in("mw", [2 * 1024, 9216])
    mb_d = dram_in("mb", [1, 2 * 9216])
    w1_d = dram_in("w1s", [4 * NQ * 128, 2048])
    w2_d = dram_in("w2s", [4 * 8 * 128, NQ * 128])
    wqk_d = dram_in("wqk", [8 * 128, 2048])
    wv_d = dram_in("wv", [4 * 128, 2048])
    wo_d = dram_in("wo", [8 * 128, 1024])
    pwi_d = dram_in("pwi", [8 * 128, 1024])
    pwg_d = dram_in("pwg", [8 * 128, 256])
    pwo_d = dram_in("pwo", [8 * 128, 1024])
    out_d = nc.dram_tensor("outT", [128, 8, SEQ], F32, kind="ExternalOutput").ap()

    w1_b = dram_tmp("w1b", [4 * NQ * 128, 2048])
    w2_b = dram_tmp("w2b", [4 * 8 * 128, NQ * 128])
    wqk_b = dram_tmp("wqkb", [8 * 128, 2048])
    wv_b = dram_tmp("wvb", [4 * 128, 2048])
    wo_b = dram_tmp("wob", [8 * 128, 1024])
    pwi_b = dram_tmp("pwib", [8 * 128, 1024])
    pwg_b = dram_tmp("pwgb", [8 * 128, 256])
    pwo_b = dram_tmp("pwob", [8 * 128, 1024])
    q_s = dram_tmp("q_s", [8 * 128, SEQ])
    k_s = dram_tmp("k_s", [8 * 128, SEQ])
    v_s = dram_tmp("v_s", [8 * 8 * 128, 512])

    es = ExitStack()
    with es:
        def sb(name, shape, dt):
            return es.enter_context(nc.sbuf_tensor("sb_" + name, list(shape), dt))

        x_sb = sb("x_sb", [128, 8, SEQ], F32)
        hT = sb("hT", [128, 8, TT], BF16)
        ringA = sb("ringA", [128, 3, 2048], BF16)
        vec = sb("vec", [128, 208], F32)
        cpk = sb("cpk", [128, 8], F32)
        cact = sb("cact", [128, 8], F32)
        modv = sb("modv", [128, 144], F32)
        av = sb("av", [128, 48], F32)
        gmv = sb("gmv", [128, 48], F32)
        ones_bf = sb("ones_bf", [128, 128], BF16)
        onesf = sb("onesf", [128, 128], F32)
        tri = sb("tri", [128, 128], BF16)
        tri2 = sb("tri2", [128, 128], BF16)
        onesb = sb("onesb", [128, 128], BF16)
        invc = sb("invc", [128, 4, 16], F32)
        junk = sb("junk", [128, 8], F32)
        epsb = sb("epsb", [128, 1], F32)
        ARENA_F32 = 14380
        arena = sb("arena", [128, ARENA_F32], F32)
        psum = es.enter_context(nc.psum_tensor("psum", [128, 8, 512], F32))

        class Carver:
            def __init__(self):
                self.off = 0

            def f32(self, n):
                a = arena[:, self.off:self.off + n]
                self.off += n
                assert self.off <= ARENA_F32, self.off
                return a

        def bf16_view(ap_f32, n_bf):
            return ap_f32.bitcast(BF16)

        sem_names = list(ENGS)
        dma_streams = ["ld", "cv", "slab", "slab2", "st", "kv", "mw", "out"]
        sems = {}
        for e in ENGS:
            sems[e] = es.enter_context(nc.semaphore("s_" + e))
        for qn_, cnt_ in Sched.DMA_SEMS.items():
            for i_ in range(cnt_):
                sems[("dma", qn_, i_)] = es.enter_context(nc.semaphore("d_%s_%d" % (qn_, i_)))

        barrier_id = [0]

        def barrier():
            barrier_id[0] += 1
            for e in ["pe", "act", "dve", "pool"]:
                pass

        ALLR = "__all__"

        def add(eng, fn, reads=(), writes=(), dma=None):
            return S.add(eng, fn, tuple(reads) + (ALLR,), writes, dma)

        def phase_barrier():
            S.add("dve", lambda e: e.memset(junk[:], 0.0), (), (ALLR,))

        def mm(out, lhsT, rhs, start, stop, reads, writes, tp=None, sgc=False):
            kw = {}
            if tp is not None:
                kw["tile_position"] = tp
            if sgc:
                kw["skip_group_check"] = True
            add("pe", lambda e: e.matmul(out, lhsT, rhs, start=start, stop=stop, **kw), reads, writes)

        def act(out, in_, func, reads, writes, bias=None, scale=None):
            kw = {}
            if bias is not None:
                kw["bias"] = bias
            if scale is not None:
                kw["scale"] = scale
            add("act", lambda e: e.activation(out=out, in_=in_, func=func, **kw), reads, writes)

        def dma(queue, out, in_, stream, reads, writes):
            add(queue, lambda e: e.dma_start(out=out, in_=in_), reads, writes, dma=stream)

        def PS(b):
            return ("ps", b)

        def load_x(T):
            dma("sp", x_sb[:, :, T * TT:(T + 1) * TT], xT_d[:, :, T * TT:(T + 1) * TT], "ld",
                (), tuple(("x", c, T) for c in range(8)))

        def setup():
            dma("sp", vec[:], vec_d, "ld", (), ("vec",))
            dma("sp", cpk[:], cpk_d, "ld", (), ("cpk",))
            load_x(0)
            add("pool", lambda e: e.memset(onesf[:], 1.0), (), ("onesf",))
            add("pool", lambda e: e.memset(epsb[:], EPS), (), ("epsb",))
            act(cact[:], cpk[:], AF.Silu, ("cpk",), ("cact",))
            add("pool", lambda e: e.memset(ones_bf[:], 1.0 / 1024.0), (), ("ones_bf",))
            add("pool", lambda e: e.memset(onesb[:], 1.0), (), ("onesb",))
            add("pool", lambda e: e.affine_select(out=tri[:], in_=onesf[:], pattern=[[-1, 128]],
                                                  compare_op=ALU.is_ge, fill=0.0, base=0,
                                                  channel_multiplier=1), ("onesf",), ("tri",))
            add("pool", lambda e: e.affine_select(out=tri2[:], in_=onesf[:], pattern=[[1, 128]],
                                                  compare_op=ALU.is_gt, fill=0.0, base=0,
                                                  channel_multiplier=-1), ("onesf",), ("tri2",))
            for g in range(4):
                add("pool", (lambda g: lambda e: e.iota(invc[:, g, :], pattern=[[1, 16]], base=1,
                                                        channel_multiplier=0,
                                                        allow_small_or_imprecise_dtypes=True))(g),
                    (), ("invc",))
                add("dve", (lambda g: lambda e: e.tensor_scalar_min(out=invc[:, g, :], in0=invc[:, g, :],
                                                                    scalar1=float(2 ** (g + 1))))(g),
                    ("invc",), ("invc",))
            add("dve", lambda e: e.reciprocal(out=invc[:], in_=invc[:]), ("invc",), ("invc",))

        def convert_weights(which, gate=None):
            jobs = []

            def cv(dst, src, r0, nrows, key):
                jobs.append((dst[r0:r0 + nrows, :], src[r0:r0 + nrows, :], key))
            if which.startswith("ffn"):
                f = int(which[3:])
                for q in range(NQ):
                    cv(w1_b, w1_d, (f * NQ + q) * 128, 128, ("w1b", f, q))
                for dt in range(8):
                    cv(w2_b, w2_d, (f * 8 + dt) * 128, 128, ("w2b", f, dt))
            elif which == "attn":
                for p in range(8):
                    cv(wqk_b, wqk_d, p * 128, 128, ("wqkb", p))
                for s2 in range(4):
                    cv(wv_b, wv_d, s2 * 128, 128, ("wvb", s2))
                for dt in range(8):
                    cv(wo_b, wo_d, dt * 128, 128, ("wob", dt))
            elif which == "pool":
                cv(pwi_b, pwi_d, 0, 1024, ("pwib",))
                cv(pwg_b, pwg_d, 0, 1024, ("pwgb",))
                cv(pwo_b, pwo_d, 0, 1024, ("pwob",))
            for n, (o_, i_, key) in enumerate(jobs):
                bgq.append((which, o_, i_, key))

        bgq = []

        def bg_emit(k):
            for _ in range(k):
                if not bgq:
                    return
                _w, o_, i_, key = bgq.pop(0)
                S.add("pool", (lambda o_, i_: lambda e: e.dma_start(out=o_, in_=i_))(o_, i_), (), (key,), dma="cv")

        def bg_require(which):
            while any(j[0] == which for j in bgq):
                bg_emit(1)

        def mod_setup(cv):
            st = {"ring": [cv.f32(4096).rearrange("p (k n) -> p k n", k=8) for _ in range(2)],
                  "row": cv.f32(512), "mbr2": [cv.f32(512), cv.f32(512)], "acc": cv.f32(512),
                  "one11": cv.f32(1), "ln": 0, "slot": {}}
            one11 = st["one11"]
            add("pool", lambda e: e.memset(one11[0:1, :], 1.0), (), ("one11",))
            return st

        def mod_load(st, i, hu):
            slot = st["ln"] % 2
            st["ln"] += 1
            st["slot"][(i, hu)] = slot
            c0 = hu * 512
            dma("sp", st["ring"][slot],
                mw_d[i * 1024:(i + 1) * 1024, c0:c0 + 512].rearrange("(k p) n -> p k n", p=128), "mw",
                (), (("mwr", slot),))
            dma("sp", st["mbr2"][slot][0:1, :], mb_d[0:1, i * 9216 + c0: i * 9216 + c0 + 512], "ld", (),
                (("mbr", slot),))

        def mod_acc(st, i, hu, k):
            slot = st["slot"][(i, hu)]
            ring, acc = st["ring"][slot], st["acc"]
            if k == 0:
                add("dve", lambda e: e.tensor_scalar(out=acc, in0=ring[:, 0, :], scalar1=cact[:, 0:1], scalar2=None,
                                                     op0=ALU.mult), (("mwr", slot), "cact"), ("macc",))
            else:
                add("dve", lambda e: e.scalar_tensor_tensor(
                    out=acc, in0=ring[:, k, :], scalar=cact[:, k:k + 1], in1=acc, op0=ALU.mult, op1=ALU.add),
                    (("mwr", slot), "cact", "macc"), ("macc",))

        def mod_rowfin(st, i, hu, bank_r):
            slot = st["slot"][(i, hu)]
            row, mbr, acc = st["row"], st["mbr2"][slot], st["acc"]
            mm(psum[0:1, bank_r, :], onesf[:, 0:1], acc, True, True, ("macc", "onesf"), (PS(bank_r),))
            add("dve", lambda e: e.tensor_tensor(out=row[0:1, :], in0=psum[0:1, bank_r, :], in1=mbr[0:1, :],
                                                 op=ALU.add), (PS(bank_r), ("mbr", slot)), ("row",))

        def mod_row(st, i, hu, bank_r):
            for k in range(8):
                mod_acc(st, i, hu, k)
            mod_rowfin(st, i, hu, bank_r)

        def mod_fin(st, i, hu, bank_t):
            row, one11 = st["row"], st["one11"]
            for cc in range(4):
                mm(psum[:, bank_t, cc:cc + 1], row[0:1, cc * 128:(cc + 1) * 128], one11[0:1, 0:1], True, True,
                   ("row", "one11"), (PS(bank_t),))
            mcol = i * 72 + hu * 4
            add("dve", lambda e: e.tensor_copy(out=modv[:, mcol:mcol + 4], in_=psum[:, bank_t, 0:4]),
                (PS(bank_t),), ("modv",))
            j = hu // 2
            sidx = j // 3
            if hu % 2 == 1 and j % 3 == 1:
                sc = modv[:, i * 72 + j * 8: i * 72 + j * 8 + 8]
                ng = vec[:, 144 + (i * 3 + sidx) * 8: 144 + (i * 3 + sidx + 1) * 8]
                a_o = av[:, (i * 3 + sidx) * 8:(i * 3 + sidx + 1) * 8]
                add("dve", lambda e: e.scalar_tensor_tensor(out=a_o, in0=sc, scalar=1.0, in1=ng, op0=ALU.add,
                                                            op1=ALU.mult), ("modv", "vec"), ("av",))
            if hu % 2 == 1 and j % 3 == 2:
                gt = modv[:, i * 72 + j * 8: i * 72 + j * 8 + 8]
                g_o = gmv[:, (i * 3 + sidx) * 8:(i * 3 + sidx + 1) * 8]
                half = 1.0 if sidx == 1 else 0.5
                add("dve", lambda e: e.tensor_scalar(out=g_o, in0=gt, scalar1=1.0, scalar2=half, op0=ALU.add,
                                                     op1=ALU.mult), ("modv",), ("gmv",))

        def mod_phase(units, gated_bg=0):
            cv = Carver()
            st = mod_setup(cv)
            units = list(units)

            def gated(u):
                slot = st["slot"][u]
                for _ in range(gated_bg):
                    if not bgq:
                        return
                    _w, o_, i_, key = bgq.pop(0)
                    S.add("pool", (lambda o_, i_: lambda e: e.dma_start(out=o_, in_=i_))(o_, i_),
                          (("mwr", slot),), (key,), dma="cv")

            if units:
                mod_load(st, *units[0])
                gated(units[0])
            for n, (i, hu) in enumerate(units):
                if n + 1 < len(units):
                    mod_load(st, *units[n + 1])
                    gated(units[n + 1])
                mod_row(st, i, hu, 2 * (n % 2))
                mod_fin(st, i, hu, 2 * (n % 2) + 1)

        def norm_tile(T, li, s, tmp, hdst=None, hkey="hT"):
            hdst = hT if hdst is None else hdst
            sq, rstd, xn = tmp["sq"], tmp["rstd"], tmp["xn"]
            ts = slice(T * TT, (T + 1) * TT)
            for c in range(8):
                b = c % len(sq)
                if c % 2 == 1:
                    act(sq[b], x_sb[:, c, ts], AF.Square, (("x", c, T),), (("sq", b),))
                else:
                    add("pool", (lambda c, b: lambda e: e.tensor_tensor(
                        out=sq[b], in0=x_sb[:, c, ts], in1=x_sb[:, c, ts], op=ALU.mult))(c, b),
                        (("x", c, T),), (("sq", b),))
                mm(psum[:, 0, :], ones_bf[:], sq[b], c == 0, c == 7, (("sq", b), "ones_bf"), (PS(0),))
            act(rstd, psum[:, 0, :], AF.Ln, (PS(0), "epsb"), ("rstd",), bias=epsb[:, 0:1])
            act(rstd, rstd, AF.Exp, ("rstd",), ("rstd",), scale=-0.5)
            if s is None:
                return
            norm_apply(T, li, s, tmp, hdst, hkey)

        def norm_apply(T, li, s, tmp, hdst=None, hkey="hT"):
            hdst = hT if hdst is None else hdst
            rstd, xn = tmp["rstd"], tmp["xn"]
            ts = slice(T * TT, (T + 1) * TT)
            sh_base = li * 72 + (3 * s) * 8
            a_base = (li * 3 + s) * 8
            for c in range(8):
                b = c % 2
                add("dve", (lambda c, b: lambda e: e.tensor_tensor(
                    out=xn[b], in0=x_sb[:, c, ts], in1=rstd, op=ALU.mult))(c, b),
                    (("x", c, T), "rstd"), (("xn", b),))
                act(hdst[:, c, :], xn[b], AF.Identity, (("xn", b), "av", "modv"), ((hkey, c),),
                    bias=modv[:, sh_base + c: sh_base + c + 1], scale=av[:, a_base + c: a_base + c + 1])

        def final_tile(T, tmp, do_norm=True):
            ts = slice(T * TT, (T + 1) * TT)
            if do_norm:
                norm_tile(T, 0, None, tmp)
                for c in range(8):
                    add("dve", (lambda c, ts: lambda e: e.scalar_tensor_tensor(
                        out=x_sb[:, c, ts], in0=x_sb[:, c, ts], scalar=vec[:, 192 + c: 193 + c],
                        in1=tmp["rstd"], op0=ALU.mult, op1=ALU.mult))(c, ts),
                        (("x", c, T), "rstd", "vec"), (("x", c, T),))
            if T == NT - 1:
                for c in range(8):
                    dma("sp", out_d[:, c, ts], x_sb[:, c, ts], "out", (("x", c, T),), (("out", T, c),))
            else:
                dma("pool", out_d[:, :, ts], x_sb[:, :, ts], "out", tuple(("x", c, T) for c in range(8)),
                    (("out", T),))

        def ffn_phase(li, j, fuse_final=False, bgn=0, xload=False):
            f = li * 2 + j
            bg_require("ffn%d" % f)
            s = 0 if j == 0 else 2
            cv = Carver()
            actT_f = cv.f32(NQ * TT // 2)
            actT = actT_f.bitcast(BF16).rearrange("p (q t) -> p q t", q=NQ)
            w2r = [cv.f32(NQ * 128 // 2).bitcast(BF16).rearrange("p (q d) -> p q d", q=NQ) for _ in range(2)]
            sg = [cv.f32(TT) for _ in range(2)]
            tmp = {"sq": [cv.f32(TT // 2).bitcast(BF16) for _ in range(8)], "rstd": cv.f32(TT),
                   "xn": [cv.f32(TT) for _ in range(2)]}
            gm_base = (li * 3 + s) * 8
            slabn = [0]

            def core_gateup(T):
                for q in range(NQ):
                    slot = slabn[0] % 3
                    slabn[0] += 1
                    r0 = (f * NQ + q) * 128
                    dma("sp", ringA[:, slot, :], w1_b[r0:r0 + 128, :], "slab", (("w1b", f, q),), (("rA", slot),))
                    sl = ringA[:, slot, :].rearrange("p (g k m) -> p g k m", g=2, k=8)
                    pb = 1 + 2 * (q % 2)
                    for g in range(2):
                        for k in range(8):
                            mm(psum[:, pb + g, :], sl[:, g, k, :], hT[:, k, :], k == 0, k == 7,
                               (("rA", slot), ("hT", k)), (PS(pb + g),))
                    b = q % 2
                    act(sg[b], psum[:, pb, :], AF.Silu, (PS(pb),), (("sg", b),))
                    add("dve", (lambda q, b, pb: lambda e: e.tensor_tensor(
                        out=actT[:, q, :], in0=psum[:, pb + 1, :], in1=sg[b], op=ALU.mult))(q, b, pb),
                        (PS(pb + 1), ("sg", b)), (("act", q),))

            def core_down(T):
                ts = slice(T * TT, (T + 1) * TT)
                for dt in range(8):
                    slot = dt % 2
                    r0 = (f * 8 + dt) * 128
                    dma("sp", w2r[slot].rearrange("p q d -> p (q d)"), w2_b[r0:r0 + 128, :], "slab2",
                        (("w2b", f, dt),), (("w2r", slot),))
                    pb = 5 + dt % 2
                    for q in range(NQ):
                        mm(psum[:, pb, :], w2r[slot][:, q, :], actT[:, q, :], q == 0, q == NQ - 1,
                           (("w2r", slot), ("act", q)), (PS(pb),))
                    add("dve", (lambda dt, pb: lambda e: e.scalar_tensor_tensor(
                        out=x_sb[:, dt, ts], in0=psum[:, pb, :], scalar=gmv[:, gm_base + dt: gm_base + dt + 1],
                        in1=x_sb[:, dt, ts], op0=ALU.mult, op1=ALU.add))(dt, pb),
                        (PS(pb), "gmv", ("x", dt, T)), (("x", dt, T),))

            norm_tile(0, li, s, tmp)
            for T in range(NT):
                if xload and T + 1 < NT:
                    load_x(T + 1)
                core_gateup(T)
                bg_emit(bgn)
                if T + 1 < NT:
                    norm_tile(T + 1, li, s, tmp)
                core_down(T)
                if fuse_final and T >= 1:
                    final_tile(T - 1, tmp)
            if fuse_final:
                final_tile(NT - 1, tmp)

        def attn_proj_phase(li, units=(), bgn=0):
            s = 1
            bg_require("attn")
            cv = Carver()
            units = list(units)
            mst = mod_setup(cv) if units else None
            per_tile = (len(units) + NT - 1) // NT
            hbuf = [(hT, "hT"), (hT, "hT")]
            tmp = {"sq": [cv.f32(TT // 2).bitcast(BF16) for _ in range(2)], "rstd": cv.f32(TT),
                   "xn": [cv.f32(TT) for _ in range(2)]}
            ev = [cv.f32(TT // 2).bitcast(BF16) for _ in range(4)]
            evv = [cv.f32(TT).bitcast(BF16) for _ in range(2)]
            evvn = [0]
            evn = [0]
            slabn = [0]
            mstate = {"cur": None, "k": 0, "nxt": None}
            if units:
                mstate["nxt"] = units.pop(0)
                mod_load(mst, *mstate["nxt"])

            def mod_step(n_ops):
                if mst is None:
                    return
                for _ in range(n_ops):
                    if mstate["cur"] is None:
                        if mstate["nxt"] is None:
                            return
                        mstate["cur"] = mstate["nxt"]
                        mstate["k"] = 0
                        mstate["nxt"] = units.pop(0) if units else None
                    i_, hu_ = mstate["cur"]
                    if mstate["k"] < 8:
                        mod_acc(mst, i_, hu_, mstate["k"])
                        mstate["k"] += 1
                        if mstate["k"] == 8:
                            return
                    else:
                        mod_rowfin(mst, i_, hu_, 7)
                        mod_fin(mst, i_, hu_, 0)
                        mstate["cur"] = None
                        if mstate["nxt"] is not None:
                            mod_load(mst, *mstate["nxt"])

            for T in range(NT):
                hcur, hk = hbuf[T % 2]
                bg_emit(bgn)
                norm_tile(T, li, s, tmp, hcur, hk)
                ts = slice(T * TT, (T + 1) * TT)
                for p in range(8):
                    mod_step(3)
                    slot = slabn[0] % 3
                    slabn[0] += 1
                    dma("sp", ringA[:, slot, :], wqk_b[p * 128:(p + 1) * 128, :], "slab", (("wqkb", p),),
                        (("rA", slot),))
                    sl = ringA[:, slot, :].rearrange("p (g k m) -> p g k m", g=2, k=8)
                    pb = 1 + 2 * (p % 2)
                    for g in range(2):
                        for k in range(8):
                            mm(psum[:, pb + g, :], sl[:, g, k, :], hcur[:, k, :], k == 0, k == 7,
                               (("rA", slot), (hk, k)), (PS(pb + g),))
                    for g in range(2):
                        b = evn[0] % 4
                        evn[0] += 1
                        dst = (q_s if g == 0 else k_s)[p * 128:(p + 1) * 128, ts]
                        if g == 0:
                            act(ev[b], psum[:, pb + g, :], AF.Copy, (PS(pb + g),), (("ev", b),))
                        else:
                            add("dve", (lambda b, pb, g: lambda e: e.tensor_copy(out=ev[b], in_=psum[:, pb + g, :]))(b, pb, g),
                                (PS(pb + g),), (("ev", b),))
                        dma("pool", dst, ev[b], "st", (("ev", b),), (("qk_s", g, p, T),))
                for s2 in range(4):
                    mod_step(3)
                    slot = slabn[0] % 3
                    slabn[0] += 1
                    dma("sp", ringA[:, slot, :], wv_b[s2 * 128:(s2 + 1) * 128, :], "slab", (("wvb", s2),),
                        (("rA", slot),))
                    sl = ringA[:, slot, :].rearrange("p (k m) -> p k m", k=8)
                    b = evvn[0] % 2
                    evvn[0] += 1
                    evt = evv[b].rearrange("p (pl kb d) -> p kb pl d", pl=2, kb=4)
                    for tt in range(4):
                        pb = 5 + tt // 2
                        c0 = (tt % 2) * 256
                        for k in range(8):
                            mm(psum[:, pb, c0:c0 + 256], hcur[:, k, tt * 128:(tt + 1) * 128], sl[:, k, :],
                               k == 0, k == 7, (("rA", slot), (hk, k)), (PS(pb),))
                    for bk in range(2):
                        src = psum[:, 5 + bk, :].rearrange("p (tl pl d) -> p tl pl d", tl=2, pl=2)
                        dstv = evt[:, 2 * bk:2 * bk + 2, :, :]
                        if bk == 0:
                            add("dve", (lambda dstv, src: lambda e: e.tensor_copy(out=dstv, in_=src))(dstv, src),
                                (PS(5 + bk),), (("evv", b),))
                        else:
                            add("act", (lambda dstv, src: lambda e: e.activation(out=dstv, in_=src, func=AF.Copy))(dstv, src),
                                (PS(5 + bk),), (("evv", b),))
                    for pl in range(2):
                        pair = 2 * s2 + pl
                        r0 = (pair * 8 + T) * 128
                        dma("pool", v_s[r0:r0 + 128, :], evv[b].rearrange("p (pl r) -> p pl r", pl=2)[:, pl, :], "st",
                            (("evv", b),), (("v_s", pair, T),))

            for _ in range(64):
                mod_step(8)

        def attn_core_phase(li, bgn=0):
            s = 1
            cv = Carver()
            NKV = 4
            ktr = [cv.f32(TT // 2).bitcast(BF16) for _ in range(NKV)]
            vtr = [cv.f32(TT // 2).bitcast(BF16).rearrange("p (kb d) -> p kb d", kb=4) for _ in range(NKV)]
            qtr = [cv.f32(TT // 2).bitcast(BF16) for _ in range(2)]
            NE = 3
            Eb = [cv.f32(2 * TT).rearrange("p (h t) -> p h t", h=2) for _ in range(NE)]
            spb = [cv.f32(TT).bitcast(BF16).rearrange("p (h t) -> p h t", h=2) for _ in range(NE)]
            Gb = [cv.f32(TT).bitcast(BF16).rearrange("p (h t) -> p h t", h=2) for _ in range(2)]
            Ab = [cv.f32(TT).bitcast(BF16).rearrange("p (h t) -> p h t", h=2) for _ in range(2)]
            oT = cv.f32(8 * TT // 2).bitcast(BF16).rearrange("p (a t) -> p a t", a=8)
            gm_base = (li * 3 + s) * 8
            slabn = [0]
            tiles = []
            steps = []
            tile_start = {}
            g = 0
            for i in range(NT):
                tile_start[i] = len(steps)
                for pair in range(8):
                    nblk = 4 * (i + 1)
                    cnt = 0
                    for kt in range(i, -1, -1):
                        slot = len(tiles) % NKV
                        tiles.append((pair, kt, slot, g, i))
                        for kb in range(3, -1, -1):
                            cnt += 1
                            steps.append(dict(pair=pair, kt=kt, kb=kb, slot=slot, g=g, i=i, first=(cnt == 1),
                                              last=(cnt == nblk), diag=(kt == i), tile=len(tiles) - 1,
                                              newtile=(kb == 3)))
                    g += 1
            NS = len(steps)
            loaded = [False] * len(tiles)
            qloaded = set()

            def load_tile(j):
                if j >= len(tiles) or loaded[j]:
                    return
                loaded[j] = True
                pair, kt, slot, g_, i_ = tiles[j]
                tsq = slice(i_ * TT, (i_ + 1) * TT)
                if g_ not in qloaded:
                    qloaded.add(g_)
                    dma("sp", qtr[g_ % 2], q_s[pair * 128:(pair + 1) * 128, tsq], "kv", (("qk_s", 0, pair, i_),),
                        (("qtr", g_ % 2),))
                dma("sp", ktr[slot], k_s[pair * 128:(pair + 1) * 128, kt * TT:(kt + 1) * TT], "kv",
                    (("qk_s", 1, pair, kt),), (("ktr", slot),))
                r0 = (pair * 8 + kt) * 128
                dma("sp", vtr[slot].rearrange("p kb d -> p (kb d)"), v_s[r0:r0 + 128, :], "kv",
                    (("v_s", pair, kt),), (("vtr", slot),))

            def oproj(i_, dt):
                tsq = slice(i_ * TT, (i_ + 1) * TT)
                slot = slabn[0] % 3
                slabn[0] += 1
                dma("sp", ringA[:, slot, 0:1024], wo_b[dt * 128:(dt + 1) * 128, :], "slab", (("wob", dt),),
                    (("rA", slot),))
                sl = ringA[:, slot, 0:1024].rearrange("p (a d) -> p a d", a=8)
                for pair in range(8):
                    mm(psum[:, 7, :], sl[:, pair, :], oT[:, pair, :], pair == 0, pair == 7,
                       (("rA", slot), ("oT", pair)), (PS(7),))
                add("dve", lambda e: e.scalar_tensor_tensor(
                    out=x_sb[:, dt, tsq], in0=psum[:, 7, :], scalar=gmv[:, gm_base + dt: gm_base + dt + 1],
                    in1=x_sb[:, dt, tsq], op0=ALU.mult, op1=ALU.add),
                    (PS(7), "gmv", ("x", dt, i_)), (("x", dt, i_),))

            oproj_at = {}
            for i in range(NT - 1):
                for dt in range(8):
                    oproj_at.setdefault(tile_start[i + 1] + 1 + dt, []).append((i, dt))
            starts = set(tile_start.values())

            def c0_of(st):
                return 128 * st["kb"] if st["diag"] else 0

            def st_qk(n):
                st = steps[n]
                if st["newtile"]:
                    load_tile(st["tile"])
                    load_tile(st["tile"] + 1)
                c0 = c0_of(st)
                kcols = slice(st["kb"] * 128, (st["kb"] + 1) * 128)
                qb = st["g"] % 2
                for h in range(2):
                    hp = slice(64 * h, 64 * h + 64)
                    mm(psum[:, h, c0:], ktr[st["slot"]][hp, kcols], qtr[qb][hp, c0:], True, True,
                       (("ktr", st["slot"]), ("qtr", qb)), (PS(h),))

            def st_esp(n):
                st = steps[n]
                b = n % NE
                c0 = c0_of(st)
                act(Eb[b][:, :, c0:], psum[:, 0:2, c0:], AF.Exp, (PS(0), PS(1)), (("E", b),), scale=0.125)
                act(spb[b][:, :, c0:], Eb[b][:, :, c0:], AF.Ln, (("E", b),), (("sp", b),), bias=1.0)
                if st["diag"]:
                    add("pool", (lambda b, c0: lambda e: e.tensor_tensor(
                        out=spb[b][:, :, c0:c0 + 128], in0=spb[b][:, :, c0:c0 + 128],
                        in1=tri2[:].unsqueeze(1).to_broadcast([128, 2, 128]), op=ALU.mult))(b, c0),
                        (("sp", b), "tri2"), (("sp", b),))

            def pbank(n):
                return 2 + 2 * (n % 2)

            def st_ones(n):
                st = steps[n]
                if st["first"]:
                    return
                sp_ = steps[n - 1]
                b = (n - 1) % NE
                c0 = c0_of(sp_)
                pb_ = pbank(n)
                for h in range(2):
                    mm(psum[:, pb_ + h, c0:], onesb[:], spb[b][:, h, c0:], sp_["first"], False,
                       (("sp", b), "onesb"), (PS(pb_ + h),), sgc=True)

            def st_tri(n):
                st = steps[n]
                b = n % NE
                c0 = c0_of(st)
                pb_ = pbank(n)
                for h in range(2):
                    mm(psum[:, pb_ + h, c0:], tri[:], spb[b][:, h, c0:], st["first"], False,
                       (("sp", b), "tri"), (PS(pb_ + h),), sgc=True)

            def st_g(n):
                st = steps[n]
                c0 = c0_of(st)
                pb_ = pbank(n)
                act(Gb[n % 2][:, :, c0:], psum[:, pb_:pb_ + 2, c0:], AF.Exp, (PS(pb_), PS(pb_ + 1)),
                    (("G", n % 2),), scale=-1.0)

            def st_tri2(n):
                st = steps[n]
                if n + 2 >= NS or steps[n + 2]["g"] != st["g"]:
                    return
                b = n % NE
                c0 = c0_of(st)
                pb_ = pbank(n)
                for h in range(2):
                    mm(psum[:, pb_ + h, c0:], tri2[:], spb[b][:, h, c0:], False, False,
                       (("sp", b), "tri2"), (PS(pb_ + h),), sgc=True)

            def st_a(n):
                st = steps[n]
                b = n % NE
                a_ = n % 2
                c0 = c0_of(st)
                add("dve", (lambda b, a_, c0: lambda e: e.tensor_tensor(
                    out=Ab[a_][:, :, c0:], in0=Eb[b][:, :, c0:], in1=Gb[a_][:, :, c0:], op=ALU.mult))(b, a_, c0),
                    (("E", b), ("G", a_)), (("A", a_),))
                if st["diag"]:
                    add("pool", (lambda a_, c0: lambda e: e.tensor_tensor(
                        out=Ab[a_][:, :, c0:c0 + 128], in0=Ab[a_][:, :, c0:c0 + 128],
                        in1=tri2[:].unsqueeze(1).to_broadcast([128, 2, 128]), op=ALU.mult))(a_, c0),
                        (("A", a_), "tri2"), (("A", a_),))

            def st_av(n):
                st = steps[n]
                a_ = n % 2
                c0 = c0_of(st)
                ob = 6
                for h in range(2):
                    hp = slice(64 * h, 64 * h + 64)
                    mm(psum[hp, ob, c0:], vtr[st["slot"]][:, st["kb"], hp], Ab[a_][:, h, c0:], st["first"], st["last"],
                       (("vtr", st["slot"]), ("A", a_)), (PS(ob),), tp=(0, 64 * h), sgc=True)
                if st["last"]:
                    pair = st["pair"]
                    add("dve", (lambda pair, ob: lambda e: e.tensor_copy(out=oT[:, pair, :], in_=psum[:, ob, :]))(pair, ob),
                        (PS(ob),), (("oT", pair),))


            for t in range(-1, NS + 1):
                if 0 <= t < NS and steps[t]["first"]:
                    bg_emit(bgn)
                for (i_, dt) in oproj_at.get(t, ()):
                    oproj(i_, dt)
                if 0 <= t + 1 < NS:
                    st_qk(t + 1)
                    st_esp(t + 1)
                if 0 <= t < NS:
                    st_ones(t)
                    st_tri(t)
                    st_g(t)
                    st_a(t)
                if 0 <= t - 1 < NS:
                    st_tri2(t - 1)
                    st_av(t - 1)
            for dt in range(8):
                oproj(NT - 1, dt)

        def pool_phase(li):
            s = 1
            bg_require("pool")
            cv = Carver()
            tmp = {"sq": [cv.f32(TT // 2).bitcast(BF16) for _ in range(8)], "rstd": cv.f32(TT),
                   "xn": [cv.f32(TT) for _ in range(2)]}
            wg = cv.f32(1024 // 2).bitcast(BF16).rearrange("p (o c e) -> p o c e", o=8, c=2) if False else None
            wg_f = cv.f32(8 * 256 // 2)
            wgs = wg_f.bitcast(BF16).rearrange("p (o ce) -> p o ce", o=8)
            W = TT + 16
            ub = [cv.f32(W) for _ in range(2)]
            lv = [cv.f32(W) for _ in range(2)]
            halo = cv.f32(8 * 16).rearrange("p (c h) -> p c h", c=8)
            pT = cv.f32(8 * TT // 2).bitcast(BF16).rearrange("p (c t) -> p c t", c=8)
            psT = cv.f32(8 * TT // 2).bitcast(BF16).rearrange("p (c t) -> p c t", c=8)
            gm_base = (li * 3 + s) * 8
            for o in range(8):
                dma("sp", wgs[:, o, :], pwg_b[o * 128:(o + 1) * 128, :], "ld", (("pwgb",),), ("wgs",))
            add("pool", lambda e: e.memset(halo, 0.0), (), ("halo",))
            slabn = [0]
            hT2p = cv.f32(8 * TT // 2).bitcast(BF16).rearrange("p (c t) -> p c t", c=8)
            hbufp = [(hT, "hT"), (hT2p, "hT2p")]
            norm_tile(0, li, s, tmp, hbufp[0][0], hbufp[0][1])
            for T in range(NT):
                ts = slice(T * TT, (T + 1) * TT)
                hcur, hk = hbufp[T % 2]
                def group_mm(g):
                    for o in (2 * g, 2 * g + 1):
                        pb = 3 + o % 2
                        wsl = wgs[:, o, :].rearrange("p (c e) -> p c e", c=2)
                        for cc in range(2):
                            mm(psum[:, pb, :], wsl[:, cc, :], pT[:, 2 * g + cc, :], cc == 0, cc == 1,
                               ("wgs", ("pT", 2 * g + cc)), (PS(pb),))
                        act(psT[:, o, :], psum[:, pb, :], AF.Identity, (PS(pb), "vec"), (("psT", o),),
                            scale=vec[:, 200 + o: 201 + o])

                for cpos, c in enumerate((7, 6, 5, 4, 3, 2, 1, 0)):
                    if cpos == 4:
                        group_mm(3)
                    if cpos == 6:
                        group_mm(2)
                    g = c // 2
                    w = 2 ** (g + 1)
                    slot = slabn[0] % 3
                    slabn[0] += 1
                    dma("sp", ringA[:, slot, 0:1024], pwi_b[c * 128:(c + 1) * 128, :], "slab", (("pwib",),),
                        (("rA", slot),))
                    sl = ringA[:, slot, 0:1024].rearrange("p (k m) -> p k m", k=8)
                    pb = 1 + cpos % 2
                    for k in range(8):
                        mm(psum[:, pb, :], sl[:, k, :], hcur[:, k, :], k == 0, k == 7,
                           (("rA", slot), (hk, k)), (PS(pb),))
                    u = ub[c % 2]
                    add("act", (lambda u, pb: lambda e: e.activation(out=u[:, 16:W], in_=psum[:, pb, :], func=AF.Copy))(u, pb),
                        (PS(pb),), (("u", c % 2),))
                    add("pool", (lambda u, c: lambda e: e.tensor_copy(out=u[:, 0:16], in_=halo[:, c, :]))(u, c),
                        ("halo",), (("u", c % 2),))
                    cur = u
                    curk = ("u", c % 2)
                    sh = 1
                    li_ = 0
                    while sh < w:
                        dst = lv[li_ % 2]
                        dk = ("lv", li_ % 2)
                        lo = 2 * sh - 1
                        add("dve", (lambda dst, cur, sh, lo: lambda e: e.tensor_tensor(
                            out=dst[:, lo:W], in0=cur[:, lo:W], in1=cur[:, lo - sh:W - sh], op=ALU.add))(dst, cur, sh, lo),
                            (curk,), (dk,))
                        cur, curk = dst, dk
                        sh *= 2
                        li_ += 1
                    add("dve", (lambda cur, u, c, w: lambda e: e.scalar_tensor_tensor(
                        out=pT[:, c, :], in0=cur[:, 16:W], scalar=1.0 / w, in1=u[:, 16:W],
                        op0=ALU.mult, op1=ALU.subtract))(cur, u, c, w),
                        (curk, ("u", c % 2)), (("pT", c),))
                    if T == 0:
                        t16 = lv[(li_) % 2]
                        add("dve", (lambda cur, g, t16: lambda e: e.tensor_tensor(
                            out=t16[:, 0:16], in0=cur[:, 16:32], in1=invc[:, g, :], op=ALU.mult))(cur, g, t16),
                            (curk, "invc"), (("lv", li_ % 2),))
                        add("dve", (lambda u, c, t16: lambda e: e.tensor_tensor(
                            out=pT[:, c, 0:16], in0=t16[:, 0:16], in1=u[:, 16:32], op=ALU.subtract))(u, c, t16),
                            (("lv", li_ % 2), ("u", c % 2)), (("pT", c),))
                    add("pool", (lambda u, c: lambda e: e.tensor_copy(out=halo[:, c, :], in_=u[:, W - 16:W]))(u, c),
                        (("u", c % 2),), ("halo",))
                group_mm(1)
                group_mm(0)
                if T + 1 < NT:
                    norm_tile(T + 1, li, s, tmp, hbufp[(T + 1) % 2][0], hbufp[(T + 1) % 2][1])
                for dt in range(8):
                    pb = 5 + dt % 2
                    slot = slabn[0] % 3
                    slabn[0] += 1
                    dma("sp", ringA[:, slot, 0:1024], pwo_b[dt * 128:(dt + 1) * 128, :], "slab", (("pwob",),),
                        (("rA", slot),))
                    wsl = ringA[:, slot, 0:1024].rearrange("p (k d) -> p k d", k=8)
                    for k in range(8):
                        mm(psum[:, pb, :], wsl[:, k, :], psT[:, k, :], k == 0, k == 7,
                           (("rA", slot), ("psT", k)), (PS(pb),))
                    add("dve", (lambda dt, pb, ts: lambda e: e.scalar_tensor_tensor(
                        out=x_sb[:, dt, ts], in0=psum[:, pb, :], scalar=gmv[:, gm_base + dt: gm_base + dt + 1],
                        in1=x_sb[:, dt, ts], op0=ALU.mult, op1=ALU.add))(dt, pb, ts),
                        (PS(pb), "gmv", ("x", dt, T)), (("x", dt, T),))

        def final_phase(do_norm=True):
            cv = Carver()
            tmp = {"sq": [cv.f32(TT // 2).bitcast(BF16) for _ in range(2)], "rstd": cv.f32(TT),
                   "xn": [cv.f32(TT) for _ in range(2)]}
            for T in range(NT):
                final_tile(T, tmp, do_norm)

        sub = sub or {}
        fused_final = False
        setup()

        def xgate(n, total):
            T = min(NT - 1, (n * NT) // total)
            return (("x", 0, T),)

        convert_weights("ffn0")
        for w_ in ("attn", "ffn1", "ffn2", "pool", "ffn3"):
            convert_weights(w_)
        if "mod" in phases:
            mod_phase([(0, hu) for hu in range(12)], gated_bg=3)
            phase_barrier()
        bg_require("ffn0")
        lazy_x = ("L0" in phases) and ("ffa" in sub.get("L0", ("ffa", "mix", "ffb")))
        if not lazy_x:
            for T_ in range(1, NT):
                load_x(T_)
        l1_units = [(0, hu) for hu in range(12, 18)] + [(1, hu) for hu in range(18)]
        if "L0" in phases:
            parts = sub.get("L0", ("ffa", "mix", "ffb"))
            if "ffa" in parts:
                ffn_phase(0, 0, bgn=3, xload=lazy_x)
                phase_barrier()
            if "mix" in parts:
                attn_proj_phase(0, l1_units, bgn=0)
                l1_units = []
                phase_barrier()
                attn_core_phase(0, bgn=2)
                phase_barrier()
            if "ffb" in parts:
                ffn_phase(0, 1, bgn=0)
                phase_barrier()
        if l1_units:
            mod_phase(l1_units)
            phase_barrier()
        if "L1" in phases:
            parts = sub.get("L1", ("ffa", "mix", "ffb"))
            if "ffa" in parts:
                ffn_phase(1, 0, bgn=0)
                phase_barrier()
            if "mix" in parts:
                pool_phase(1)
                phase_barrier()
            if "ffb" in parts:
                fused_final = ("final" in phases)
                ffn_phase(1, 1, fuse_final=fused_final)
                phase_barrier()
        if not fused_final:
            final_phase(do_norm=("final" in phases))
        S.add("sp", lambda e: e.nop(), tuple(("out", T) for T in range(NT - 1)) +
              tuple(("out", NT - 1, c) for c in range(8)), ())

        S.finalize()
        block = es.enter_context(nc.Block())

        @block.tensor
        def _(e):
            S.emit_engine("pe", e, sems)

        @block.scalar
        def _(e):
            S.emit_engine("act", e, sems)

        @block.vector
        def _(e):
            S.emit_engine("dve", e, sems)

        @block.gpsimd
        def _(e):
            S.emit_engine("pool", e, sems)

        @block.sync
        def _(e):
            S.emit_engine("sp", e, sems)

    return nc


def prep_shared(mod_w, mod_b, norm_g, ffn_w1, ffn_w2, attn_w_in, attn_w_out, pool_w_in, pool_w_group,
                pool_scale, pool_w_out, final_norm):
    f = np.float32
    mw = np.ascontiguousarray(np.asarray(mod_w, f).reshape(2 * 1024, 9216))
    mb = np.ascontiguousarray(np.asarray(mod_b, f).reshape(1, 2 * 9216))
    vec = np.zeros((128, 208), f)
    vec[:, 0:144] = np.asarray(mod_b, f).reshape(2, 72, 128).transpose(2, 0, 1).reshape(128, 144)
    vec[:, 144:192] = np.asarray(norm_g, f).reshape(2, 3, 8, 128).transpose(3, 0, 1, 2).reshape(128, 48)
    vec[:, 192:200] = np.asarray(final_norm, f).reshape(8, 128).T
    vec[:, 200:208] = np.asarray(pool_scale, f).reshape(8, 128).T
    w1 = np.asarray(ffn_w1, f).reshape(4, 8, 128, 2, NQ, 128)
    w1s = np.ascontiguousarray(w1.transpose(0, 4, 2, 3, 1, 5)).reshape(4 * NQ * 128, 2048)
    w2 = np.asarray(ffn_w2, f).reshape(4, NQ, 128, 8, 128)
    w2s = np.ascontiguousarray(w2.transpose(0, 3, 2, 1, 4)).reshape(4 * 8 * 128, NQ * 128)
    wi = np.asarray(attn_w_in, f)[0].reshape(8, 128, 3, 8, 128)
    wqk = np.ascontiguousarray(wi[:, :, 0:2].transpose(3, 1, 2, 0, 4)).reshape(8 * 128, 2048)
    wvv = np.asarray(attn_w_in, f)[0][:, 2048:3072].reshape(8, 128, 4, 256)
    wv = np.ascontiguousarray(wvv.transpose(2, 1, 0, 3)).reshape(4 * 128, 2048)
    wo_ = np.asarray(attn_w_out, f)[0].reshape(8, 128, 8, 128)
    wo = np.ascontiguousarray(wo_.transpose(2, 1, 0, 3)).reshape(8 * 128, 1024)
    pi = np.asarray(pool_w_in, f)[0].reshape(8, 128, 8, 128)
    pwi = np.ascontiguousarray(pi.transpose(2, 1, 0, 3)).reshape(8 * 128, 1024)
    pg = np.asarray(pool_w_group, f)[0].reshape(4, 2, 128, 2, 128)
    pwg = np.ascontiguousarray(pg.transpose(0, 3, 2, 1, 4)).reshape(8 * 128, 256)
    po = np.asarray(pool_w_out, f)[0].reshape(8, 128, 8, 128)
    pwo = np.ascontiguousarray(po.transpose(2, 1, 0, 3)).reshape(8 * 128, 1024)
    return {"mw": mw, "mb": mb, "vec": vec, "w1s": w1s, "w2s": w2s, "wqk": wqk, "wv": wv, "wo": wo,
            "pwi": pwi, "pwg": pwg, "pwo": pwo}


def x_to_dev(xb):
    return np.ascontiguousarray(np.asarray(xb, np.float32).T.reshape(8, 128, SEQ).transpose(1, 0, 2))


def x_from_dev(o):
    return np.ascontiguousarray(o.transpose(1, 0, 2).reshape(D, SEQ).T)


_NC_CACHE = {}


def kernel(x, c, mod_w, mod_b, norm_g, ffn_w1, ffn_w2, attn_w_in, attn_w_out,
           pool_w_in, pool_w_group, pool_scale, pool_w_out, final_norm):
    shared = prep_shared(mod_w, mod_b, norm_g, ffn_w1, ffn_w2, attn_w_in, attn_w_out, pool_w_in,
                         pool_w_group, pool_scale, pool_w_out, final_norm)
    x = np.asarray(x, np.float32)
    c = np.asarray(c, np.float32)
    in_maps = []
    for b in range(NCORE):
        m = dict(shared)
        m["xT"] = x_to_dev(x[b])
        m["cpk"] = np.ascontiguousarray(c[b].reshape(8, 128).T)
        in_maps.append(m)
    nc = build_program()
    res = run_bass_kernel_spmd(nc, in_maps, core_ids=list(range(NCORE)))
    out = np.stack([x_from_dev(np.asarray(r["outT"])) for r in res.results], axis=0)
    return out.astype(np.float32)
```

```python
import numpy as np
from contextlib import ExitStack
import concourse.bass as bass
import concourse.mybir as mybir
from concourse.bass_utils import run_bass_kernel_spmd

F32 = mybir.dt.float32
BF16 = mybir.dt.bfloat16
AF = mybir.ActivationFunctionType
ALU = mybir.AluOpType

D = 1024
SEQ = 4096
NCORE = 8
DFF = 2816
NQ = DFF // 128
TT = 512
NT = SEQ // TT
EPS = 1e-6
ENGS = ["pe", "act", "dve", "pool", "sp"]


class Op:
    __slots__ = ("eng", "fn", "deps", "sig", "val", "semkey", "ordidx", "isdma")


class Sched:
    DMA_SEMS = {"sp": 24, "pool": 8, "act": 4}

    def __init__(self):
        self.dmacount = {}
        self.q = {e: [] for e in ENGS}
        self.semlist = {}
        self.lastw = {}
        self.rd = {}

    def add(self, eng, fn, reads=(), writes=(), dma=None):
        op = Op()
        op.eng = eng
        op.fn = fn
        op.isdma = dma is not None
        if dma is not None:
            n = self.dmacount.get(eng, 0)
            self.dmacount[eng] = n + 1
            op.semkey = ("dma", eng, n % self.DMA_SEMS[eng])
        else:
            op.semkey = eng
        op.sig = False
        op.val = 0
        deps = {}

        def need(o):
            if o is None:
                return
            if o.semkey == "pe" and op.semkey == "pe":
                return
            k = o.semkey
            if k not in deps or deps[k].ordidx < o.ordidx:
                deps[k] = o

        if op.isdma and self.semlist.get(op.semkey):
            need(self.semlist[op.semkey][-1])
        for r in reads:
            need(self.lastw.get(r))
        for w in writes:
            need(self.lastw.get(w))
            for o in self.rd.get(w, {}).values():
                need(o)
        lst = self.semlist.setdefault(op.semkey, [])
        op.ordidx = len(lst)
        lst.append(op)
        for o in deps.values():
            o.sig = True
        op.deps = list(deps.values())
        for r in reads:
            self.rd.setdefault(r, {})[op.semkey] = op
        for w in writes:
            self.lastw[w] = op
            self.rd[w] = {}
        self.q[eng].append(op)
        return op

    def finalize(self):
        for k, lst in self.semlist.items():
            if isinstance(k, tuple):
                for i, o in enumerate(lst):
                    o.val = 16 * (i + 1)
            else:
                n = 0
                for o in lst:
                    if o.sig:
                        n += 1
                        o.val = n

    def emit_engine(self, eng, e, sems):
        waited = {}
        for op in self.q[eng]:
            for d in op.deps:
                k = d.semkey
                if waited.get(k, 0) >= d.val:
                    continue
                waited[k] = d.val
                e.wait_ge(sems[k], d.val)
            ins = op.fn(e)
            if op.isdma:
                ins.then_inc(sems[op.semkey], 16)
            elif op.sig:
                ins.then_inc(sems[op.semkey], 1)


def build_program(phases=("mod", "L0", "L1", "final"), sub=None):
    nc = bass.Bass("TRN2", target_bir_lowering=False)
    S = Sched()

    def dram_in(name, shape, dt=F32):
        return nc.dram_tensor(name, list(shape), dt, kind="ExternalInput").ap()

    def dram_tmp(name, shape, dt=BF16):
        return nc.dram_tensor(name, list(shape), dt, kind="Internal").ap()

    xT_d = dram_in("xT", [128, 8, SEQ])
    cpk_d = dram_in("cpk", [128, 8])
    vec_d = dram_in("vec", [128, 208])
    mw_d = dram_in("mw", [2 * 1024, 9216])
    mb_d = dram_in("mb", [1, 2 * 9216])
    w1_d = dram_in("w1s", [4 * NQ * 128, 2048])
    w2_d = dram_in("w2s", [4 * 8 * 128, NQ * 128])
    wqk_d = dram_in("wqk", [8 * 128, 2048])
    wv_d = dram_in("wv", [4 * 128, 2048])
    wo_d = dram_in("wo", [8 * 128, 1024])
    pwi_d = dram_in("pwi", [8 * 128, 1024])
    pwg_d = dram_in("pwg", [8 * 128, 256])
    pwo_d = dram_in("pwo", [8 * 128, 1024])
    out_d = nc.dram_tensor("outT", [128, 8, SEQ], F32, kind="ExternalOutput").ap()

    w1_b = dram_tmp("w1b", [4 * NQ * 128, 2048])
    w2_b = dram_tmp("w2b", [4 * 8 * 128, NQ * 128])
    wqk_b = dram_tmp("wqkb", [8 * 128, 2048])
    wv_b = dram_tmp("wvb", [4 * 128, 2048])
    wo_b = dram_tmp("wob", [8 * 128, 1024])
    pwi_b = dram_tmp("pwib", [8 * 128, 1024])
    pwg_b = dram_tmp("pwgb", [8 * 128, 256])
    pwo_b = dram_tmp("pwob", [8 * 128, 1024])
    q_s = dram_tmp("q_s", [8 * 128, SEQ])
    k_s = dram_tmp("k_s", [8 * 128, SEQ])
    v_s = dram_tmp("v_s", [8 * 8 * 128, 512])

    es = ExitStack()
    with es:
        def sb(name, shape, dt):
            return es.enter_context(nc.sbuf_tensor("sb_" + name, list(shape), dt))

        x_sb = sb("x_sb", [128, 8, SEQ], F32)
        hT = sb("hT", [128, 8, TT], BF16)
        ringA = sb("ringA", [128, 3, 2048], BF16)
        vec = sb("vec", [128, 208], F32)
        cpk = sb("cpk", [128, 8], F32)
        cact = sb("cact", [128, 8], F32)
        modv = sb("modv", [128, 144], F32)
        av = sb("av", [128, 48], F32)
        gmv = sb("gmv", [128, 48], F32)
        ones_bf = sb("ones_bf", [128, 128], BF16)
        onesf = sb("onesf", [128, 128], F32)
        tri = sb("tri", [128, 128], BF16)
        tri2 = sb("tri2", [128, 128], BF16)
        onesb = sb("onesb", [128, 128], BF16)
        invc = sb("invc", [128, 4, 16], F32)
        junk = sb("junk", [128, 8], F32)
        epsb = sb("epsb", [128, 1], F32)
        ARENA_F32 = 14380
        arena = sb("arena", [128, ARENA_F32], F32)
        psum = es.enter_context(nc.psum_tensor("psum", [128, 8, 512], F32))

        class Carver:
            def __init__(self):
                self.off = 0

            def f32(self, n):
                a = arena[:, self.off:self.off + n]
                self.off += n
                assert self.off <= ARENA_F32, self.off
                return a

        def bf16_view(ap_f32, n_bf):
            return ap_f32.bitcast(BF16)

        sem_names = list(ENGS)
        dma_streams = ["ld", "cv", "slab", "slab2", "st", "kv", "mw", "out"]
        sems = {}
        for e in ENGS:
            sems[e] = es.enter_context(nc.semaphore("s_" + e))
        for qn_, cnt_ in Sched.DMA_SEMS.items():
            for i_ in range(cnt_):
                sems[("dma", qn_, i_)] = es.enter_context(nc.semaphore("d_%s_%d" % (qn_, i_)))

        barrier_id = [0]

        def barrier():
            barrier_id[0] += 1
            for e in ["pe", "act", "dve", "pool"]:
                pass

        ALLR = "__all__"

        def add(eng, fn, reads=(), writes=(), dma=None):
            return S.add(eng, fn, tuple(reads) + (ALLR,), writes, dma)

        def phase_barrier():
            S.add("dve", lambda e: e.memset(junk[:], 0.0), (), (ALLR,))

        def mm(out, lhsT, rhs, start, stop, reads, writes, tp=None, sgc=False):
            kw = {}
            if tp is not None:
                kw["tile_position"] = tp
            if sgc:
                kw["skip_group_check"] = True
            add("pe", lambda e: e.matmul(out, lhsT, rhs, start=start, stop=stop, **kw), reads, writes)

        def act(out, in_, func, reads, writes, bias=None, scale=None):
            kw = {}
            if bias is not None:
                kw["bias"] = bias
            if scale is not None:
                kw["scale"] = scale
            add("act", lambda e: e.activation(out=out, in_=in_, func=func, **kw), reads, writes)

        def dma(queue, out, in_, stream, reads, writes):
            add(queue, lambda e: e.dma_start(out=out, in_=in_), reads, writes, dma=stream)

        def PS(b):
            return ("ps", b)

        def load_x(T):
            dma("sp", x_sb[:, :, T * TT:(T + 1) * TT], xT_d[:, :, T * TT:(T + 1) * TT], "ld",
                (), tuple(("x", c, T) for c in range(8)))

        def setup():
            dma("sp", vec[:], vec_d, "ld", (), ("vec",))
            dma("sp", cpk[:], cpk_d, "ld", (), ("cpk",))
            load_x(0)
            add("pool", lambda e: e.memset(onesf[:], 1.0), (), ("onesf",))
            add("pool", lambda e: e.memset(epsb[:], EPS), (), ("epsb",))
            act(cact[:], cpk[:], AF.Silu, ("cpk",), ("cact",))
            add("pool", lambda e: e.memset(ones_bf[:], 1.0 / 1024.0), (), ("ones_bf",))
            add("pool", lambda e: e.memset(onesb[:], 1.0), (), ("onesb",))
            add("pool", lambda e: e.affine_select(out=tri[:], in_=onesf[:], pattern=[[-1, 128]],
                                                  compare_op=ALU.is_ge, fill=0.0, base=0,
                                                  channel_multiplier=1), ("onesf",), ("tri",))
            add("pool", lambda e: e.affine_select(out=tri2[:], in_=onesf[:], pattern=[[1, 128]],
                                                  compare_op=ALU.is_gt, fill=0.0, base=0,
                                                  channel_multiplier=-1), ("onesf",), ("tri2",))
            for g in range(4):
                add("pool", (lambda g: lambda e: e.iota(invc[:, g, :], pattern=[[1, 16]], base=1,
                                                        channel_multiplier=0,
                                                        allow_small_or_imprecise_dtypes=True))(g),
                    (), ("invc",))
                add("dve", (lambda g: lambda e: e.tensor_scalar_min(out=invc[:, g, :], in0=invc[:, g, :],
                                                                    scalar1=float(2 ** (g + 1))))(g),
                    ("invc",), ("invc",))
            add("dve", lambda e: e.reciprocal(out=invc[:], in_=invc[:]), ("invc",), ("invc",))

        def convert_weights(which, gate=None):
            jobs = []

            def cv(dst, src, r0, nrows, key):
                jobs.append((dst[r0:r0 + nrows, :], src[r0:r0 + nrows, :], key))
            if which.startswith("ffn"):
                f = int(which[3:])
                for q in range(NQ):
                    cv(w1_b, w1_d, (f * NQ + q) * 128, 128, ("w1b", f, q))
                for dt in range(8):
                    cv(w2_b, w2_d, (f * 8 + dt) * 128, 128, ("w2b", f, dt))
            elif which == "attn":
                for p in range(8):
                    cv(wqk_b, wqk_d, p * 128, 128, ("wqkb", p))
                for s2 in range(4):
                    cv(wv_b, wv_d, s2 * 128, 128, ("wvb", s2))
                for dt in range(8):
                    cv(wo_b, wo_d, dt * 128, 128, ("wob", dt))
            elif which == "pool":
                cv(pwi_b, pwi_d, 0, 1024, ("pwib",))
                cv(pwg_b, pwg_d, 0, 1024, ("pwgb",))
                cv(pwo_b, pwo_d, 0, 1024, ("pwob",))
            for n, (o_, i_, key) in enumerate(jobs):
                bgq.append((which, o_, i_, key))

        bgq = []

        def bg_emit(k):
            for _ in range(k):
                if not bgq:
                    return
                _w, o_, i_, key = bgq.pop(0)
                S.add("pool", (lambda o_, i_: lambda e: e.dma_start(out=o_, in_=i_))(o_, i_), (), (key,), dma="cv")

        def bg_require(which):
            while any(j[0] == which for j in bgq):
                bg_emit(1)

        def mod_setup(cv):
            st = {"ring": [cv.f32(4096).rearrange("p (k n) -> p k n", k=8) for _ in range(2)],
                  "row": cv.f32(512), "mbr2": [cv.f32(512), cv.f32(512)], "acc": cv.f32(512),
                  "one11": cv.f32(1), "ln": 0, "slot": {}}
            one11 = st["one11"]
            add("pool", lambda e: e.memset(one11[0:1, :], 1.0), (), ("one11",))
            return st

        def mod_load(st, i, hu):
            slot = st["ln"] % 2
            st["ln"] += 1
            st["slot"][(i, hu)] = slot
            c0 = hu * 512
            dma("sp", st["ring"][slot],
                mw_d[i * 1024:(i + 1) * 1024, c0:c0 + 512].rearrange("(k p) n -> p k n", p=128), "mw",
                (), (("mwr", slot),))
            dma("sp", st["mbr2"][slot][0:1, :], mb_d[0:1, i * 9216 + c0: i * 9216 + c0 + 512], "ld", (),
                (("mbr", slot),))

        def mod_acc(st, i, hu, k):
            slot = st["slot"][(i, hu)]
            ring, acc = st["ring"][slot], st["acc"]
            if k == 0:
                add("dve", lambda e: e.tensor_scalar(out=acc, in0=ring[:, 0, :], scalar1=cact[:, 0:1], scalar2=None,
                                                     op0=ALU.mult), (("mwr", slot), "cact"), ("macc",))
            else:
                add("dve", lambda e: e.scalar_tensor_tensor(
                    out=acc, in0=ring[:, k, :], scalar=cact[:, k:k + 1], in1=acc, op0=ALU.mult, op1=ALU.add),
                    (("mwr", slot), "cact", "macc"), ("macc",))

        def mod_rowfin(st, i, hu, bank_r):
            slot = st["slot"][(i, hu)]
            row, mbr, acc = st["row"], st["mbr2"][slot], st["acc"]
            mm(psum[0:1, bank_r, :], onesf[:, 0:1], acc, True, True, ("macc", "onesf"), (PS(bank_r),))
            add("dve", lambda e: e.tensor_tensor(out=row[0:1, :], in0=psum[0:1, bank_r, :], in1=mbr[0:1, :],
                                                 op=ALU.add), (PS(bank_r), ("mbr", slot)), ("row",))

        def mod_row(st, i, hu, bank_r):
            for k in range(8):
                mod_acc(st, i, hu, k)
            mod_rowfin(st, i, hu, bank_r)

        def mod_fin(st, i, hu, bank_t):
            row, one11 = st["row"], st["one11"]
            for cc in range(4):
                mm(psum[:, bank_t, cc:cc + 1], row[0:1, cc * 128:(cc + 1) * 128], one11[0:1, 0:1], True, True,
                   ("row", "one11"), (PS(bank_t),))
            mcol = i * 72 + hu * 4
            add("dve", lambda e: e.tensor_copy(out=modv[:, mcol:mcol + 4], in_=psum[:, bank_t, 0:4]),
                (PS(bank_t),), ("modv",))
            j = hu // 2
            sidx = j // 3
            if hu % 2 == 1 and j % 3 == 1:
                sc = modv[:, i * 72 + j * 8: i * 72 + j * 8 + 8]
                ng = vec[:, 144 + (i * 3 + sidx) * 8: 144 + (i * 3 + sidx + 1) * 8]
                a_o = av[:, (i * 3 + sidx) * 8:(i * 3 + sidx + 1) * 8]
                add("dve", lambda e: e.scalar_tensor_tensor(out=a_o, in0=sc, scalar=1.0, in1=ng, op0=ALU.add,
                                                            op1=ALU.mult), ("modv", "vec"), ("av",))
            if hu % 2 == 1 and j % 3 == 2:
                gt = modv[:, i * 72 + j * 8: i * 72 + j * 8 + 8]
                g_o = gmv[:, (i * 3 + sidx) * 8:(i * 3 + sidx + 1) * 8]
                half = 1.0 if sidx == 1 else 0.5
                add("dve", lambda e: e.tensor_scalar(out=g_o, in0=gt, scalar1=1.0, scalar2=half, op0=ALU.add,
                                                     op1=ALU.mult), ("modv",), ("gmv",))

        def mod_phase(units, gated_bg=0):
            cv = Carver()
            st = mod_setup(cv)
            units = list(units)

            def gated(u):
                slot = st["slot"][u]
                for _ in range(gated_bg):
                    if not bgq:
                        return
                    _w, o_, i_, key = bgq.pop(0)
                    S.add("pool", (lambda o_, i_: lambda e: e.dma_start(out=o_, in_=i_))(o_, i_),
                          (("mwr", slot),), (key,), dma="cv")

            if units:
                mod_load(st, *units[0])
                gated(units[0])
            for n, (i, hu) in enumerate(units):
                if n + 1 < len(units):
                    mod_load(st, *units[n + 1])
                    gated(units[n + 1])
                mod_row(st, i, hu, 2 * (n % 2))
                mod_fin(st, i, hu, 2 * (n % 2) + 1)

        def norm_tile(T, li, s, tmp, hdst=None, hkey="hT"):
            hdst = hT if hdst is None else hdst
            sq, rstd, xn = tmp["sq"], tmp["rstd"], tmp["xn"]
            ts = slice(T * TT, (T + 1) * TT)
            for c in range(8):
                b = c % len(sq)
                if c % 3 != 0:
                    act(sq[b], x_sb[:, c, ts], AF.Square, (("x", c, T),), (("sq", b),))
                else:
                    add("pool", (lambda c, b: lambda e: e.tensor_tensor(
                        out=sq[b], in0=x_sb[:, c, ts], in1=x_sb[:, c, ts], op=ALU.mult))(c, b),
                        (("x", c, T),), (("sq", b),))
                mm(psum[:, 0, :], ones_bf[:], sq[b], c == 0, c == 7, (("sq", b), "ones_bf"), (PS(0),))
            act(rstd, psum[:, 0, :], AF.Ln, (PS(0), "epsb"), ("rstd",), bias=epsb[:, 0:1])
            act(rstd, rstd, AF.Exp, ("rstd",), ("rstd",), scale=-0.5)
            if s is None:
                return
            norm_apply(T, li, s, tmp, hdst, hkey)

        def norm_apply(T, li, s, tmp, hdst=None, hkey="hT"):
            hdst = hT if hdst is None else hdst
            rstd, xn = tmp["rstd"], tmp["xn"]
            ts = slice(T * TT, (T + 1) * TT)
            sh_base = li * 72 + (3 * s) * 8
            a_base = (li * 3 + s) * 8
            for c in range(8):
                b = c % 2
                add("dve", (lambda c, b: lambda e: e.tensor_tensor(
                    out=xn[b], in0=x_sb[:, c, ts], in1=rstd, op=ALU.mult))(c, b),
                    (("x", c, T), "rstd"), (("xn", b),))
                act(hdst[:, c, :], xn[b], AF.Identity, (("xn", b), "av", "modv"), ((hkey, c),),
                    bias=modv[:, sh_base + c: sh_base + c + 1], scale=av[:, a_base + c: a_base + c + 1])

        def final_tile(T, tmp, do_norm=True):
            ts = slice(T * TT, (T + 1) * TT)
            if do_norm:
                norm_tile(T, 0, None, tmp)
                for c in range(8):
                    add("dve", (lambda c, ts: lambda e: e.scalar_tensor_tensor(
                        out=x_sb[:, c, ts], in0=x_sb[:, c, ts], scalar=vec[:, 192 + c: 193 + c],
                        in1=tmp["rstd"], op0=ALU.mult, op1=ALU.mult))(c, ts),
                        (("x", c, T), "rstd", "vec"), (("x", c, T),))
            if T >= NT - 2:
                for c in range(8):
                    dma("sp", out_d[:, c, ts], x_sb[:, c, ts], "out", (("x", c, T),), (("out", T, c),))
            else:
                dma("pool", out_d[:, :, ts], x_sb[:, :, ts], "out", tuple(("x", c, T) for c in range(8)),
                    (("out", T),))

        def ffn_phase(li, j, fuse_final=False, bgn=0, xload=False):
            f = li * 2 + j
            bg_require("ffn%d" % f)
            s = 0 if j == 0 else 2
            cv = Carver()
            actT_f = cv.f32(NQ * TT // 2)
            actT = actT_f.bitcast(BF16).rearrange("p (q t) -> p q t", q=NQ)
            w2r = [cv.f32(NQ * 128 // 2).bitcast(BF16).rearrange("p (q d) -> p q d", q=NQ) for _ in range(2)]
            sg = [cv.f32(TT) for _ in range(2)]
            tmp = {"sq": [cv.f32(TT // 2).bitcast(BF16) for _ in range(8)], "rstd": cv.f32(TT),
                   "xn": [cv.f32(TT) for _ in range(2)]}
            gm_base = (li * 3 + s) * 8
            slabn = [0]

            def core_gateup(T):
                for q in range(NQ):
                    slot = slabn[0] % 3
                    slabn[0] += 1
                    r0 = (f * NQ + q) * 128
                    dma("sp", ringA[:, slot, :], w1_b[r0:r0 + 128, :], "slab", (("w1b", f, q),), (("rA", slot),))
                    sl = ringA[:, slot, :].rearrange("p (g k m) -> p g k m", g=2, k=8)
                    pb = 1 + 2 * (q % 2)
                    for g in range(2):
                        for k in range(8):
                            mm(psum[:, pb + g, :], sl[:, g, k, :], hT[:, k, :], k == 0, k == 7,
                               (("rA", slot), ("hT", k)), (PS(pb + g),))
                    b = q % 2
                    act(sg[b], psum[:, pb, :], AF.Silu, (PS(pb),), (("sg", b),))
                    add("dve", (lambda q, b, pb: lambda e: e.tensor_tensor(
                        out=actT[:, q, :], in0=psum[:, pb + 1, :], in1=sg[b], op=ALU.mult))(q, b, pb),
                        (PS(pb + 1), ("sg", b)), (("act", q),))

            def core_down(T):
                ts = slice(T * TT, (T + 1) * TT)
                for dt in range(8):
                    slot = dt % 2
                    r0 = (f * 8 + dt) * 128
                    dma("sp", w2r[slot].rearrange("p q d -> p (q d)"), w2_b[r0:r0 + 128, :], "slab2",
                        (("w2b", f, dt),), (("w2r", slot),))
                    pb = 5 + dt % 2
                    for q in range(NQ):
                        mm(psum[:, pb, :], w2r[slot][:, q, :], actT[:, q, :], q == 0, q == NQ - 1,
                           (("w2r", slot), ("act", q)), (PS(pb),))
                    add("dve", (lambda dt, pb: lambda e: e.scalar_tensor_tensor(
                        out=x_sb[:, dt, ts], in0=psum[:, pb, :], scalar=gmv[:, gm_base + dt: gm_base + dt + 1],
                        in1=x_sb[:, dt, ts], op0=ALU.mult, op1=ALU.add))(dt, pb),
                        (PS(pb), "gmv", ("x", dt, T)), (("x", dt, T),))

            norm_tile(0, li, s, tmp)
            for T in range(NT):
                if xload and T + 1 < NT:
                    load_x(T + 1)
                core_gateup(T)
                bg_emit(bgn)
                if T + 1 < NT:
                    norm_tile(T + 1, li, s, tmp)
                core_down(T)
                if fuse_final and T >= 1:
                    final_tile(T - 1, tmp)
            if fuse_final:
                final_tile(NT - 1, tmp)

        def attn_proj_phase(li, units=(), bgn=0):
            s = 1
            bg_require("attn")
            cv = Carver()
            units = list(units)
            mst = mod_setup(cv) if units else None
            per_tile = (len(units) + NT - 1) // NT
            hbuf = [(hT, "hT"), (hT, "hT")]
            tmp = {"sq": [cv.f32(TT // 2).bitcast(BF16) for _ in range(2)], "rstd": cv.f32(TT),
                   "xn": [cv.f32(TT) for _ in range(2)]}
            ev = [cv.f32(TT // 2).bitcast(BF16) for _ in range(4)]
            evv = [cv.f32(TT).bitcast(BF16) for _ in range(2)]
            evvn = [0]
            evn = [0]
            slabn = [0]
            mstate = {"cur": None, "k": 0, "nxt": None}
            if units:
                mstate["nxt"] = units.pop(0)
                mod_load(mst, *mstate["nxt"])

            def mod_step(n_ops):
                if mst is None:
                    return
                for _ in range(n_ops):
                    if mstate["cur"] is None:
                        if mstate["nxt"] is None:
                            return
                        mstate["cur"] = mstate["nxt"]
                        mstate["k"] = 0
                        mstate["nxt"] = units.pop(0) if units else None
                    i_, hu_ = mstate["cur"]
                    if mstate["k"] < 8:
                        mod_acc(mst, i_, hu_, mstate["k"])
                        mstate["k"] += 1
                        if mstate["k"] == 8:
                            return
                    else:
                        mod_rowfin(mst, i_, hu_, 7)
                        mod_fin(mst, i_, hu_, 0)
                        mstate["cur"] = None
                        if mstate["nxt"] is not None:
                            mod_load(mst, *mstate["nxt"])

            for T in range(NT):
                hcur, hk = hbuf[T % 2]
                bg_emit(bgn)
                norm_tile(T, li, s, tmp, hcur, hk)
                ts = slice(T * TT, (T + 1) * TT)
                for p in range(8):
                    mod_step(3)
                    slot = slabn[0] % 3
                    slabn[0] += 1
                    dma("sp", ringA[:, slot, :], wqk_b[p * 128:(p + 1) * 128, :], "slab", (("wqkb", p),),
                        (("rA", slot),))
                    sl = ringA[:, slot, :].rearrange("p (g k m) -> p g k m", g=2, k=8)
                    pb = 1 + 2 * (p % 2)
                    for g in range(2):
                        for k in range(8):
                            mm(psum[:, pb + g, :], sl[:, g, k, :], hcur[:, k, :], k == 0, k == 7,
                               (("rA", slot), (hk, k)), (PS(pb + g),))
                    for g in range(2):
                        b = evn[0] % 4
                        evn[0] += 1
                        dst = (q_s if g == 0 else k_s)[p * 128:(p + 1) * 128, ts]
                        if g == 0:
                            act(ev[b], psum[:, pb + g, :], AF.Copy, (PS(pb + g),), (("ev", b),))
                        else:
                            add("dve", (lambda b, pb, g: lambda e: e.tensor_copy(out=ev[b], in_=psum[:, pb + g, :]))(b, pb, g),
                                (PS(pb + g),), (("ev", b),))
                        dma("pool", dst, ev[b], "st", (("ev", b),), (("qk_s", g, p, T),))
                for s2 in range(4):
                    mod_step(3)
                    slot = slabn[0] % 3
                    slabn[0] += 1
                    dma("sp", ringA[:, slot, :], wv_b[s2 * 128:(s2 + 1) * 128, :], "slab", (("wvb", s2),),
                        (("rA", slot),))
                    sl = ringA[:, slot, :].rearrange("p (k m) -> p k m", k=8)
                    b = evvn[0] % 2
                    evvn[0] += 1
                    evt = evv[b].rearrange("p (pl kb d) -> p kb pl d", pl=2, kb=4)
                    for tt in range(4):
                        pb = 5 + tt // 2
                        c0 = (tt % 2) * 256
                        for k in range(8):
                            mm(psum[:, pb, c0:c0 + 256], hcur[:, k, tt * 128:(tt + 1) * 128], sl[:, k, :],
                               k == 0, k == 7, (("rA", slot), (hk, k)), (PS(pb),))
                    for bk in range(2):
                        src = psum[:, 5 + bk, :].rearrange("p (tl pl d) -> p tl pl d", tl=2, pl=2)
                        dstv = evt[:, 2 * bk:2 * bk + 2, :, :]
                        if bk == 0:
                            add("dve", (lambda dstv, src: lambda e: e.tensor_copy(out=dstv, in_=src))(dstv, src),
                                (PS(5 + bk),), (("evv", b),))
                        else:
                            add("act", (lambda dstv, src: lambda e: e.activation(out=dstv, in_=src, func=AF.Copy))(dstv, src),
                                (PS(5 + bk),), (("evv", b),))
                    for pl in range(2):
                        pair = 2 * s2 + pl
                        r0 = (pair * 8 + T) * 128
                        dma("pool", v_s[r0:r0 + 128, :], evv[b].rearrange("p (pl r) -> p pl r", pl=2)[:, pl, :], "st",
                            (("evv", b),), (("v_s", pair, T),))

            for _ in range(64):
                mod_step(8)

        def attn_core_phase(li, bgn=0):
            s = 1
            cv = Carver()
            NKV = 4
            ktr = [cv.f32(TT // 2).bitcast(BF16) for _ in range(NKV)]
            vtr = [cv.f32(TT // 2).bitcast(BF16).rearrange("p (kb d) -> p kb d", kb=4) for _ in range(NKV)]
            qtr = [cv.f32(TT // 2).bitcast(BF16) for _ in range(2)]
            NE = 3
            Eb = [cv.f32(2 * TT).rearrange("p (h t) -> p h t", h=2) for _ in range(NE)]
            spb = [cv.f32(TT).bitcast(BF16).rearrange("p (h t) -> p h t", h=2) for _ in range(NE)]
            Gb = [cv.f32(TT).bitcast(BF16).rearrange("p (h t) -> p h t", h=2) for _ in range(2)]
            Ab = [cv.f32(TT).bitcast(BF16).rearrange("p (h t) -> p h t", h=2) for _ in range(2)]
            oT = cv.f32(8 * TT // 2).bitcast(BF16).rearrange("p (a t) -> p a t", a=8)
            gm_base = (li * 3 + s) * 8
            slabn = [0]
            tiles = []
            steps = []
            tile_start = {}
            g = 0
            for i in range(NT):
                tile_start[i] = len(steps)
                for pair in range(8):
                    nblk = 4 * (i + 1)
                    cnt = 0
                    for kt in range(i, -1, -1):
                        slot = len(tiles) % NKV
                        tiles.append((pair, kt, slot, g, i))
                        for kb in range(3, -1, -1):
                            cnt += 1
                            steps.append(dict(pair=pair, kt=kt, kb=kb, slot=slot, g=g, i=i, first=(cnt == 1),
                                              last=(cnt == nblk), diag=(kt == i), tile=len(tiles) - 1,
                                              newtile=(kb == 3)))
                    g += 1
            NS = len(steps)
            loaded = [False] * len(tiles)
            qloaded = set()

            def load_tile(j):
                if j >= len(tiles) or loaded[j]:
                    return
                loaded[j] = True
                pair, kt, slot, g_, i_ = tiles[j]
                tsq = slice(i_ * TT, (i_ + 1) * TT)
                if g_ not in qloaded:
                    qloaded.add(g_)
                    dma("sp", qtr[g_ % 2], q_s[pair * 128:(pair + 1) * 128, tsq], "kv", (("qk_s", 0, pair, i_),),
                        (("qtr", g_ % 2),))
                dma("sp", ktr[slot], k_s[pair * 128:(pair + 1) * 128, kt * TT:(kt + 1) * TT], "kv",
                    (("qk_s", 1, pair, kt),), (("ktr", slot),))
                r0 = (pair * 8 + kt) * 128
                dma("sp", vtr[slot].rearrange("p kb d -> p (kb d)"), v_s[r0:r0 + 128, :], "kv",
                    (("v_s", pair, kt),), (("vtr", slot),))

            def oproj(i_, dt):
                tsq = slice(i_ * TT, (i_ + 1) * TT)
                slot = slabn[0] % 3
                slabn[0] += 1
                dma("sp", ringA[:, slot, 0:1024], wo_b[dt * 128:(dt + 1) * 128, :], "slab", (("wob", dt),),
                    (("rA", slot),))
                sl = ringA[:, slot, 0:1024].rearrange("p (a d) -> p a d", a=8)
                for pair in range(8):
                    mm(psum[:, 7, :], sl[:, pair, :], oT[:, pair, :], pair == 0, pair == 7,
                       (("rA", slot), ("oT", pair)), (PS(7),))
                add("dve", lambda e: e.scalar_tensor_tensor(
                    out=x_sb[:, dt, tsq], in0=psum[:, 7, :], scalar=gmv[:, gm_base + dt: gm_base + dt + 1],
                    in1=x_sb[:, dt, tsq], op0=ALU.mult, op1=ALU.add),
                    (PS(7), "gmv", ("x", dt, i_)), (("x", dt, i_),))

            oproj_at = {}
            for i in range(NT - 1):
                for dt in range(8):
                    oproj_at.setdefault(tile_start[i + 1] + 1 + dt, []).append((i, dt))
            starts = set(tile_start.values())

            def c0_of(st):
                return 128 * st["kb"] if st["diag"] else 0

            def st_qk(n):
                st = steps[n]
                if st["newtile"]:
                    load_tile(st["tile"])
                    load_tile(st["tile"] + 1)
                c0 = c0_of(st)
                kcols = slice(st["kb"] * 128, (st["kb"] + 1) * 128)
                qb = st["g"] % 2
                for h in range(2):
                    hp = slice(64 * h, 64 * h + 64)
                    mm(psum[:, h, c0:], ktr[st["slot"]][hp, kcols], qtr[qb][hp, c0:], True, True,
                       (("ktr", st["slot"]), ("qtr", qb)), (PS(h),))

            def st_esp(n):
                st = steps[n]
                b = n % NE
                c0 = c0_of(st)
                act(Eb[b][:, :, c0:], psum[:, 0:2, c0:], AF.Exp, (PS(0), PS(1)), (("E", b),), scale=0.125)
                act(spb[b][:, :, c0:], Eb[b][:, :, c0:], AF.Ln, (("E", b),), (("sp", b),), bias=1.0)
                if st["diag"]:
                    add("pool", (lambda b, c0: lambda e: e.tensor_tensor(
                        out=spb[b][:, :, c0:c0 + 128], in0=spb[b][:, :, c0:c0 + 128],
                        in1=tri2[:].unsqueeze(1).to_broadcast([128, 2, 128]), op=ALU.mult))(b, c0),
                        (("sp", b), "tri2"), (("sp", b),))

            def pbank(n):
                return 2 + 2 * (n % 2)

            def st_ones(n):
                st = steps[n]
                if st["first"]:
                    return
                sp_ = steps[n - 1]
                b = (n - 1) % NE
                c0 = c0_of(sp_)
                pb_ = pbank(n)
                for h in range(2):
                    mm(psum[:, pb_ + h, c0:], onesb[:], spb[b][:, h, c0:], sp_["first"], False,
                       (("sp", b), "onesb"), (PS(pb_ + h),), sgc=True)

            def st_tri(n):
                st = steps[n]
                b = n % NE
                c0 = c0_of(st)
                pb_ = pbank(n)
                for h in range(2):
                    mm(psum[:, pb_ + h, c0:], tri[:], spb[b][:, h, c0:], st["first"], False,
                       (("sp", b), "tri"), (PS(pb_ + h),), sgc=True)

            def st_g(n):
                st = steps[n]
                c0 = c0_of(st)
                pb_ = pbank(n)
                act(Gb[n % 2][:, :, c0:], psum[:, pb_:pb_ + 2, c0:], AF.Exp, (PS(pb_), PS(pb_ + 1)),
                    (("G", n % 2),), scale=-1.0)

            def st_tri2(n):
                st = steps[n]
                if n + 2 >= NS or steps[n + 2]["g"] != st["g"]:
                    return
                b = n % NE
                c0 = c0_of(st)
                pb_ = pbank(n)
                for h in range(2):
                    mm(psum[:, pb_ + h, c0:], tri2[:], spb[b][:, h, c0:], False, False,
                       (("sp", b), "tri2"), (PS(pb_ + h),), sgc=True)

            def st_a(n):
                st = steps[n]
                b = n % NE
                a_ = n % 2
                c0 = c0_of(st)
                add("dve", (lambda b, a_, c0: lambda e: e.tensor_tensor(
                    out=Ab[a_][:, :, c0:], in0=Eb[b][:, :, c0:], in1=Gb[a_][:, :, c0:], op=ALU.mult))(b, a_, c0),
                    (("E", b), ("G", a_)), (("A", a_),))
                if st["diag"]:
                    add("pool", (lambda a_, c0: lambda e: e.tensor_tensor(
                        out=Ab[a_][:, :, c0:c0 + 128], in0=Ab[a_][:, :, c0:c0 + 128],
                        in1=tri2[:].unsqueeze(1).to_broadcast([128, 2, 128]), op=ALU.mult))(a_, c0),
                        (("A", a_), "tri2"), (("A", a_),))

            def st_av(n):
                st = steps[n]
                a_ = n % 2
                c0 = c0_of(st)
                ob = 6
                for h in range(2):
                    hp = slice(64 * h, 64 * h + 64)
                    mm(psum[hp, ob, c0:], vtr[st["slot"]][:, st["kb"], hp], Ab[a_][:, h, c0:], st["first"], st["last"],
                       (("vtr", st["slot"]), ("A", a_)), (PS(ob),), tp=(0, 64 * h), sgc=True)
                if st["last"]:
                    pair = st["pair"]
                    add("dve", (lambda pair, ob: lambda e: e.tensor_copy(out=oT[:, pair, :], in_=psum[:, ob, :]))(pair, ob),
                        (PS(ob),), (("oT", pair),))


            for t in range(-1, NS + 1):
                if 0 <= t < NS and steps[t]["first"]:
                    bg_emit(bgn)
                for (i_, dt) in oproj_at.get(t, ()):
                    oproj(i_, dt)
                if 0 <= t + 1 < NS:
                    st_qk(t + 1)
                    st_esp(t + 1)
                if 0 <= t < NS:
                    st_ones(t)
                    st_tri(t)
                    st_g(t)
                    st_a(t)
                if 0 <= t - 1 < NS:
                    st_tri2(t - 1)
                    st_av(t - 1)
            for dt in range(8):
                oproj(NT - 1, dt)

        def pool_phase(li):
            s = 1
            bg_require("pool")
            cv = Carver()
            tmp = {"sq": [cv.f32(TT // 2).bitcast(BF16) for _ in range(8)], "rstd": cv.f32(TT),
                   "xn": [cv.f32(TT) for _ in range(2)]}
            wg = cv.f32(1024 // 2).bitcast(BF16).rearrange("p (o c e) -> p o c e", o=8, c=2) if False else None
            wg_f = cv.f32(8 * 256 // 2)
            wgs = wg_f.bitcast(BF16).rearrange("p (o ce) -> p o ce", o=8)
            W = TT + 16
            ub = [cv.f32(W) for _ in range(2)]
            lv = [cv.f32(W) for _ in range(2)]
            halo = cv.f32(8 * 16).rearrange("p (c h) -> p c h", c=8)
            pT = cv.f32(8 * TT // 2).bitcast(BF16).rearrange("p (c t) -> p c t", c=8)
            psT = cv.f32(8 * TT // 2).bitcast(BF16).rearrange("p (c t) -> p c t", c=8)
            gm_base = (li * 3 + s) * 8
            for o in range(8):
                dma("sp", wgs[:, o, :], pwg_b[o * 128:(o + 1) * 128, :], "ld", (("pwgb",),), ("wgs",))
            add("pool", lambda e: e.memset(halo, 0.0), (), ("halo",))
            slabn = [0]
            hT2p = cv.f32(8 * TT // 2).bitcast(BF16).rearrange("p (c t) -> p c t", c=8)
            hbufp = [(hT, "hT"), (hT2p, "hT2p")]
            norm_tile(0, li, s, tmp, hbufp[0][0], hbufp[0][1])
            for T in range(NT):
                ts = slice(T * TT, (T + 1) * TT)
                hcur, hk = hbufp[T % 2]
                def group_mm(g):
                    for o in (2 * g, 2 * g + 1):
                        pb = 3 + o % 2
                        wsl = wgs[:, o, :].rearrange("p (c e) -> p c e", c=2)
                        for cc in range(2):
                            mm(psum[:, pb, :], wsl[:, cc, :], pT[:, 2 * g + cc, :], cc == 0, cc == 1,
                               ("wgs", ("pT", 2 * g + cc)), (PS(pb),))
                        act(psT[:, o, :], psum[:, pb, :], AF.Identity, (PS(pb), "vec"), (("psT", o),),
                            scale=vec[:, 200 + o: 201 + o])

                for cpos, c in enumerate((7, 6, 5, 4, 3, 2, 1, 0)):
                    if cpos == 4:
                        group_mm(3)
                    if cpos == 6:
                        group_mm(2)
                    g = c // 2
                    w = 2 ** (g + 1)
                    slot = slabn[0] % 3
                    slabn[0] += 1
                    dma("sp", ringA[:, slot, 0:1024], pwi_b[c * 128:(c + 1) * 128, :], "slab", (("pwib",),),
                        (("rA", slot),))
                    sl = ringA[:, slot, 0:1024].rearrange("p (k m) -> p k m", k=8)
                    pb = 1 + cpos % 2
                    for k in range(8):
                        mm(psum[:, pb, :], sl[:, k, :], hcur[:, k, :], k == 0, k == 7,
                           (("rA", slot), (hk, k)), (PS(pb),))
                    u = ub[c % 2]
                    add("act", (lambda u, pb: lambda e: e.activation(out=u[:, 16:W], in_=psum[:, pb, :], func=AF.Copy))(u, pb),
                        (PS(pb),), (("u", c % 2),))
                    add("pool", (lambda u, c: lambda e: e.tensor_copy(out=u[:, 0:16], in_=halo[:, c, :]))(u, c),
                        ("halo",), (("u", c % 2),))
                    cur = u
                    curk = ("u", c % 2)
                    sh = 1
                    li_ = 0
                    while sh < w:
                        dst = lv[li_ % 2]
                        dk = ("lv", li_ % 2)
                        lo = 2 * sh - 1
                        add("dve", (lambda dst, cur, sh, lo: lambda e: e.tensor_tensor(
                            out=dst[:, lo:W], in0=cur[:, lo:W], in1=cur[:, lo - sh:W - sh], op=ALU.add))(dst, cur, sh, lo),
                            (curk,), (dk,))
                        cur, curk = dst, dk
                        sh *= 2
                        li_ += 1
                    add("dve", (lambda cur, u, c, w: lambda e: e.scalar_tensor_tensor(
                        out=pT[:, c, :], in0=cur[:, 16:W], scalar=1.0 / w, in1=u[:, 16:W],
                        op0=ALU.mult, op1=ALU.subtract))(cur, u, c, w),
                        (curk, ("u", c % 2)), (("pT", c),))
                    if T == 0:
                        t16 = lv[(li_) % 2]
                        add("dve", (lambda cur, g, t16: lambda e: e.tensor_tensor(
                            out=t16[:, 0:16], in0=cur[:, 16:32], in1=invc[:, g, :], op=ALU.mult))(cur, g, t16),
                            (curk, "invc"), (("lv", li_ % 2),))
                        add("dve", (lambda u, c, t16: lambda e: e.tensor_tensor(
                            out=pT[:, c, 0:16], in0=t16[:, 0:16], in1=u[:, 16:32], op=ALU.subtract))(u, c, t16),
                            (("lv", li_ % 2), ("u", c % 2)), (("pT", c),))
                    add("pool", (lambda u, c: lambda e: e.tensor_copy(out=halo[:, c, :], in_=u[:, W - 16:W]))(u, c),
                        (("u", c % 2),), ("halo",))
                group_mm(1)
                group_mm(0)
                if T + 1 < NT:
                    norm_tile(T + 1, li, s, tmp, hbufp[(T + 1) % 2][0], hbufp[(T + 1) % 2][1])
                for dt in range(8):
                    pb = 5 + dt % 2
                    slot = slabn[0] % 3
                    slabn[0] += 1
                    dma("sp", ringA[:, slot, 0:1024], pwo_b[dt * 128:(dt + 1) * 128, :], "slab", (("pwob",),),
                        (("rA", slot),))
                    wsl = ringA[:, slot, 0:1024].rearrange("p (k d) -> p k d", k=8)
                    for k in range(8):
                        mm(psum[:, pb, :], wsl[:, k, :], psT[:, k, :], k == 0, k == 7,
                           (("rA", slot), ("psT", k)), (PS(pb),))
                    add("dve", (lambda dt, pb, ts: lambda e: e.scalar_tensor_tensor(
                        out=x_sb[:, dt, ts], in0=psum[:, pb, :], scalar=gmv[:, gm_base + dt: gm_base + dt + 1],
                        in1=x_sb[:, dt, ts], op0=ALU.mult, op1=ALU.add))(dt, pb, ts),
                        (PS(pb), "gmv", ("x", dt, T)), (("x", dt, T),))

        def final_phase(do_norm=True):
            cv = Carver()
            tmp = {"sq": [cv.f32(TT // 2).bitcast(BF16) for _ in range(2)], "rstd": cv.f32(TT),
                   "xn": [cv.f32(TT) for _ in range(2)]}
            for T in range(NT):
                final_tile(T, tmp, do_norm)

        sub = sub or {}
        fused_final = False
        setup()

        def xgate(n, total):
            T = min(NT - 1, (n * NT) // total)
            return (("x", 0, T),)

        convert_weights("ffn0")
        for w_ in ("attn", "ffn1", "ffn2", "pool", "ffn3"):
            convert_weights(w_)
        if "mod" in phases:
            mod_phase([(0, hu) for hu in range(12)], gated_bg=3)
            phase_barrier()
        bg_require("ffn0")
        lazy_x = ("L0" in phases) and ("ffa" in sub.get("L0", ("ffa", "mix", "ffb")))
        if not lazy_x:
            for T_ in range(1, NT):
                load_x(T_)
        l1_units = [(0, hu) for hu in range(12, 18)] + [(1, hu) for hu in range(18)]
        if "L0" in phases:
            parts = sub.get("L0", ("ffa", "mix", "ffb"))
            if "ffa" in parts:
                ffn_phase(0, 0, bgn=3, xload=lazy_x)
                phase_barrier()
            if "mix" in parts:
                attn_proj_phase(0, l1_units, bgn=0)
                l1_units = []
                phase_barrier()
                attn_core_phase(0, bgn=2)
                phase_barrier()
            if "ffb" in parts:
                ffn_phase(0, 1, bgn=0)
                phase_barrier()
        if l1_units:
            mod_phase(l1_units)
            phase_barrier()
        if "L1" in phases:
            parts = sub.get("L1", ("ffa", "mix", "ffb"))
            if "ffa" in parts:
                ffn_phase(1, 0, bgn=0)
                phase_barrier()
            if "mix" in parts:
                pool_phase(1)
                phase_barrier()
            if "ffb" in parts:
                fused_final = ("final" in phases)
                ffn_phase(1, 1, fuse_final=fused_final)
                phase_barrier()
        if not fused_final:
            final_phase(do_norm=("final" in phases))
        S.add("sp", lambda e: e.nop(), tuple(("out", T) for T in range(NT - 2)) +
              tuple(("out", T, c) for T in (NT - 2, NT - 1) for c in range(8)), ())

        S.finalize()
        block = es.enter_context(nc.Block())

        @block.tensor
        def _(e):
            S.emit_engine("pe", e, sems)

        @block.scalar
        def _(e):
            S.emit_engine("act", e, sems)

        @block.vector
        def _(e):
            S.emit_engine("dve", e, sems)

        @block.gpsimd
        def _(e):
            S.emit_engine("pool", e, sems)

        @block.sync
        def _(e):
            S.emit_engine("sp", e, sems)

    return nc


def prep_shared(mod_w, mod_b, norm_g, ffn_w1, ffn_w2, attn_w_in, attn_w_out, pool_w_in, pool_w_group,
                pool_scale, pool_w_out, final_norm):
    f = np.float32
    mw = np.ascontiguousarray(np.asarray(mod_w, f).reshape(2 * 1024, 9216))
    mb = np.ascontiguousarray(np.asarray(mod_b, f).reshape(1, 2 * 9216))
    vec = np.zeros((128, 208), f)
    vec[:, 0:144] = np.asarray(mod_b, f).reshape(2, 72, 128).transpose(2, 0, 1).reshape(128, 144)
    vec[:, 144:192] = np.asarray(norm_g, f).reshape(2, 3, 8, 128).transpose(3, 0, 1, 2).reshape(128, 48)
    vec[:, 192:200] = np.asarray(final_norm, f).reshape(8, 128).T
    vec[:, 200:208] = np.asarray(pool_scale, f).reshape(8, 128).T
    w1 = np.asarray(ffn_w1, f).reshape(4, 8, 128, 2, NQ, 128)
    w1s = np.ascontiguousarray(w1.transpose(0, 4, 2, 3, 1, 5)).reshape(4 * NQ * 128, 2048)
    w2 = np.asarray(ffn_w2, f).reshape(4, NQ, 128, 8, 128)
    w2s = np.ascontiguousarray(w2.transpose(0, 3, 2, 1, 4)).reshape(4 * 8 * 128, NQ * 128)
    wi = np.asarray(attn_w_in, f)[0].reshape(8, 128, 3, 8, 128)
    wqk = np.ascontiguousarray(wi[:, :, 0:2].transpose(3, 1, 2, 0, 4)).reshape(8 * 128, 2048)
    wvv = np.asarray(attn_w_in, f)[0][:, 2048:3072].reshape(8, 128, 4, 256)
    wv = np.ascontiguousarray(wvv.transpose(2, 1, 0, 3)).reshape(4 * 128, 2048)
    wo_ = np.asarray(attn_w_out, f)[0].reshape(8, 128, 8, 128)
    wo = np.ascontiguousarray(wo_.transpose(2, 1, 0, 3)).reshape(8 * 128, 1024)
    pi = np.asarray(pool_w_in, f)[0].reshape(8, 128, 8, 128)
    pwi = np.ascontiguousarray(pi.transpose(2, 1, 0, 3)).reshape(8 * 128, 1024)
    pg = np.asarray(pool_w_group, f)[0].reshape(4, 2, 128, 2, 128)
    pwg = np.ascontiguousarray(pg.transpose(0, 3, 2, 1, 4)).reshape(8 * 128, 256)
    po = np.asarray(pool_w_out, f)[0].reshape(8, 128, 8, 128)
    pwo = np.ascontiguousarray(po.transpose(2, 1, 0, 3)).reshape(8 * 128, 1024)
    return {"mw": mw, "mb": mb, "vec": vec, "w1s": w1s, "w2s": w2s, "wqk": wqk, "wv": wv, "wo": wo,
            "pwi": pwi, "pwg": pwg, "pwo": pwo}


def x_to_dev(xb):
    return np.ascontiguousarray(np.asarray(xb, np.float32).T.reshape(8, 128, SEQ).transpose(1, 0, 2))


def x_from_dev(o):
    return np.ascontiguousarray(o.transpose(1, 0, 2).reshape(D, SEQ).T)


_NC_CACHE = {}


def kernel(x, c, mod_w, mod_b, norm_g, ffn_w1, ffn_w2, attn_w_in, attn_w_out,
           pool_w_in, pool_w_group, pool_scale, pool_w_out, final_norm):
    shared = prep_shared(mod_w, mod_b, norm_g, ffn_w1, ffn_w2, attn_w_in, attn_w_out, pool_w_in,
                         pool_w_group, pool_scale, pool_w_out, final_norm)
    x = np.asarray(x, np.float32)
    c = np.asarray(c, np.float32)
    in_maps = []
    for b in range(NCORE):
        m = dict(shared)
        m["xT"] = x_to_dev(x[b])
        m["cpk"] = np.ascontiguousarray(c[b].reshape(8, 128).T)
        in_maps.append(m)
    nc = build_program()
    res = run_bass_kernel_spmd(nc, in_maps, core_ids=list(range(NCORE)))
    out = np.stack([x_from_dev(np.asarray(r["outT"])) for r in res.results], axis=0)
    return out.astype(np.float32)
```
